# Optimizing a Trainium2 kernel written in Bass

```python
import jax, jax.numpy as jnp
from jax import lax
import numpy as np

D_MODEL = 4096
BATCH = 8
SEQ = 2048
DEPTH = 4

HEAD_DIM = 128
MIX_WIDTH = D_MODEL
N_GROUPS = 4
GROUP_WIDTH = MIX_WIDTH // N_GROUPS
NA_HEADS = GROUP_WIDTH // HEAD_DIM
GRID_W = 64
NA_ROWS_MAX = 8
NA_COLS = 16
CONV_CHANNELS = GROUP_WIDTH
CONV_WIDTH = 31
SWA_HEADS = GROUP_WIDTH // HEAD_DIM
SWA_KV_HEADS = 2
SWA_RADIUS = 128
DIL_HEADS = GROUP_WIDTH // HEAD_DIM
DIL_PATTERNS = ((128, 1), (512, 4), (2048, 16))
FFN_DIM = D_MODEL * 3 // 2
KV_WIDTH = SWA_KV_HEADS * HEAD_DIM
SPLIT_SIZES = (GROUP_WIDTH, GROUP_WIDTH, GROUP_WIDTH,
               2 * CONV_CHANNELS,
               GROUP_WIDTH, KV_WIDTH, KV_WIDTH,
               GROUP_WIDTH, GROUP_WIDTH, GROUP_WIDTH)
IN_COLS = sum(SPLIT_SIZES)
EPS = 1e-6
NEG = -1e30

kernel_name = "hybrid_parallel_group_encoder"


def rms_norm(x, g):
    xf = x.astype(jnp.float32)
    y = xf * lax.rsqrt(jnp.mean(xf * xf, axis=-1, keepdims=True) + EPS)
    return (y * g.astype(jnp.float32)).astype(x.dtype)


def swiglu(h, w_in, w_out):
    gate, up = jnp.split(h @ w_in, 2, axis=-1)
    return (jax.nn.silu(gate) * up) @ w_out


def split_heads(t, n_heads):
    n, s, _ = t.shape
    return t.reshape(n, s, n_heads, -1).transpose(0, 2, 1, 3)


def merge_heads(t):
    n, h, s, d = t.shape
    return t.transpose(0, 2, 1, 3).reshape(n, s, h * d)


def alibi_slopes(n):
    return 2.0 ** (-8.0 * jnp.arange(1, n + 1, dtype=jnp.float32) / n)


def banded_attention(q, k, v, slopes, radius, spacing):
    n, hq, L, hd = q.shape
    hkv = k.shape[1]
    g = hq // hkv
    blk = radius
    nb = -(-L // blk)
    lp = nb * blk
    qb = jnp.pad(q, ((0, 0), (0, 0), (0, lp - L), (0, 0))).reshape(n, hkv, g, nb, blk, hd)

    def key_blocks(t):
        tb = jnp.pad(t, ((0, 0), (0, 0), (blk, lp - L + blk), (0, 0))).reshape(n, hkv, nb + 2, blk, hd)
        return jnp.concatenate([tb[:, :, :-2], tb[:, :, 1:-1], tb[:, :, 2:]], axis=3)

    kb, vb = key_blocks(k), key_blocks(v)
    s = jnp.einsum('nkgiqd,nkisd->nkgiqs', qb, kb).astype(jnp.float32) * (hd ** -0.5)
    qpos = jnp.arange(nb)[:, None] * blk + jnp.arange(blk)[None, :]
    kpos = jnp.arange(nb)[:, None] * blk - blk + jnp.arange(3 * blk)[None, :]
    rel = kpos[:, None, :] - qpos[:, :, None]
    valid = (jnp.abs(rel) <= radius) & (kpos[:, None, :] >= 0) & (kpos[:, None, :] < L)
    dist = (spacing * jnp.abs(rel)).astype(jnp.float32)
    bias = -slopes.astype(jnp.float32).reshape(hkv, g, 1, 1, 1) * dist
    s = jnp.where(valid, s + bias, NEG)
    m = jnp.max(s, axis=-1, keepdims=True)
    p = jnp.exp(s - m)
    den = jnp.sum(p, axis=-1, keepdims=True)
    o = jnp.einsum('nkgiqs,nkisd->nkgiqd', p.astype(v.dtype), vb).astype(jnp.float32) / den
    lse = (m + jnp.log(den))[..., 0]
    o = o.reshape(n, hq, lp, hd)[:, :, :L]
    lse = lse.reshape(n, hq, lp)[:, :, :L]
    return o, lse


def neighbourhood_attention(q, k, v, rpb):
    n, h, s_len, hd = q.shape
    rows = s_len // GRID_W
    kh = min(NA_ROWS_MAX, rows)
    kw = NA_COLS
    grid = lambda t: t.reshape(n, h, rows, GRID_W, hd)
    r = jnp.arange(rows)
    row_idx = jnp.clip(r - kh // 2, 0, rows - kh)[:, None] + jnp.arange(kh)[None, :]
    kr = grid(k)[:, :, row_idx]
    vr = grid(v)[:, :, row_idx]
    c = jnp.arange(GRID_W)
    cstart = jnp.clip(c - kw // 2, 0, GRID_W - kw)
    col_ok = (c[None, :] >= cstart[:, None]) & (c[None, :] < cstart[:, None] + kw)
    dr = row_idx - r[:, None]
    dc = c[None, :] - c[:, None]
    bias = rpb[:, (dr + NA_ROWS_MAX - 1)[:, None, :, None],
               jnp.clip(dc + NA_COLS - 1, 0, 2 * NA_COLS - 2)[None, :, None, :]]
    s = jnp.einsum('nhrqd,nhrkwd->nhrqkw', grid(q), kr).astype(jnp.float32) * (hd ** -0.5)
    s = jnp.where(col_ok[:, None, :], s + bias.astype(jnp.float32), NEG)
    p = jax.nn.softmax(s.reshape(n, h, rows, GRID_W, kh * GRID_W), axis=-1)
    p = p.reshape(n, h, rows, GRID_W, kh, GRID_W)
    o = jnp.einsum('nhrqkw,nhrkwd->nhrqd', p.astype(v.dtype), vr)
    return o.reshape(n, h, s_len, hd)


def conformer_conv(u, w_dw, b_dw, ln_g, ln_b):
    a, gate = jnp.split(u, 2, axis=-1)
    h = a * jax.nn.sigmoid(gate)
    h = lax.conv_general_dilated(h, w_dw[:, None, :], window_strides=(1,),
                                 padding=[(CONV_WIDTH // 2, CONV_WIDTH // 2)],
                                 dimension_numbers=('NWC', 'WIO', 'NWC'),
                                 feature_group_count=CONV_CHANNELS) + b_dw
    hf = h.astype(jnp.float32)
    mu = jnp.mean(hf, axis=-1, keepdims=True)
    var = jnp.mean(jnp.square(hf - mu), axis=-1, keepdims=True)
    hf = (hf - mu) * lax.rsqrt(var + EPS) * ln_g.astype(jnp.float32) + ln_b.astype(jnp.float32)
    return jax.nn.silu(hf).astype(u.dtype)


def dilated_attention(q, k, v, slopes):
    n, h, s_len, hd = q.shape
    outs, lses = [], []
    for window, dil in DIL_PATTERNS:
        radius = (window // 2) // dil
        sub = s_len // dil
        to_sub = lambda t: t.reshape(n, h, sub, dil, hd).transpose(0, 3, 1, 2, 4).reshape(n * dil, h, sub, hd)
        o, lse = banded_attention(to_sub(q), to_sub(k), to_sub(v), slopes, radius, dil)
        outs.append(o.reshape(n, dil, h, sub, hd).transpose(0, 2, 3, 1, 4).reshape(n, h, s_len, hd))
        lses.append(lse.reshape(n, dil, h, sub).transpose(0, 2, 3, 1).reshape(n, h, s_len))
    w = jax.nn.softmax(jnp.stack(lses), axis=0)
    return jnp.sum(w[..., None] * jnp.stack(outs), axis=0)


def token_mixing(hn, w_in, w_out, rpb, conv_w, conv_b, conv_ln_g, conv_ln_b, sink, branch_g, sl_c, sl_d):
    u = hn @ w_in
    idx = [int(i) for i in np.cumsum(SPLIT_SIZES)[:-1]]
    aq, ak, av, bu, cq, ck, cv, dq, dk, dv = jnp.split(u, idx, axis=-1)
    oa = merge_heads(neighbourhood_attention(split_heads(aq, NA_HEADS), split_heads(ak, NA_HEADS),
                                             split_heads(av, NA_HEADS), rpb))
    ob = conformer_conv(bu, conv_w, conv_b, conv_ln_g, conv_ln_b)
    oc, lse_c = banded_attention(split_heads(cq, SWA_HEADS), split_heads(ck, SWA_KV_HEADS),
                                 split_heads(cv, SWA_KV_HEADS), sl_c, SWA_RADIUS, 1)
    oc = oc * jax.nn.sigmoid(lse_c - sink.astype(jnp.float32)[None, :, None])[..., None]
    oc = merge_heads(oc)
    od = merge_heads(dilated_attention(split_heads(dq, DIL_HEADS), split_heads(dk, DIL_HEADS),
                                       split_heads(dv, DIL_HEADS), sl_d))
    groups = [oa, ob, oc, od]
    merged = jnp.concatenate([rms_norm(o.astype(hn.dtype), branch_g[i]) for i, o in enumerate(groups)], axis=-1)
    return merged @ w_out


def setup_inputs(seed: int = 0) -> dict:
    key = jax.random.key(seed)
    ks = jax.random.split(key, 20)
    nrm = lambda k, shape, scale: jax.random.normal(k, shape, jnp.float32) * scale
    gain = lambda k, shape: 1.0 + 0.02 * jax.random.normal(k, shape, jnp.float32)
    return {
        "x": nrm(ks[0], (BATCH, SEQ, D_MODEL), 1.0),
        "ffn1_norm": gain(ks[1], (DEPTH, D_MODEL)),
        "ffn1_w_in": nrm(ks[2], (DEPTH, D_MODEL, 2 * FFN_DIM), D_MODEL ** -0.5),
        "ffn1_w_out": nrm(ks[3], (DEPTH, FFN_DIM, D_MODEL), FFN_DIM ** -0.5),
        "mix_norm": gain(ks[4], (DEPTH, D_MODEL)),
        "w_in": nrm(ks[5], (DEPTH, D_MODEL, IN_COLS), D_MODEL ** -0.5),
        "na_rpb": nrm(ks[6], (DEPTH, NA_HEADS, 2 * NA_ROWS_MAX - 1, 2 * NA_COLS - 1), 0.1),
        "conv_w": nrm(ks[7], (DEPTH, CONV_WIDTH, CONV_CHANNELS), CONV_WIDTH ** -0.5),
        "conv_b": nrm(ks[8], (DEPTH, CONV_CHANNELS), 0.02),
        "conv_ln_g": gain(ks[9], (DEPTH, CONV_CHANNELS)),
        "conv_ln_b": nrm(ks[10], (DEPTH, CONV_CHANNELS), 0.02),
        "swa_sink": nrm(ks[11], (DEPTH, SWA_HEADS), 0.5),
        "branch_norm": gain(ks[12], (DEPTH, N_GROUPS, GROUP_WIDTH)),
        "w_out": nrm(ks[13], (DEPTH, MIX_WIDTH, D_MODEL), MIX_WIDTH ** -0.5),
        "ffn2_norm": gain(ks[14], (DEPTH, D_MODEL)),
        "ffn2_w_in": nrm(ks[15], (DEPTH, D_MODEL, 2 * FFN_DIM), D_MODEL ** -0.5),
        "ffn2_w_out": nrm(ks[16], (DEPTH, FFN_DIM, D_MODEL), FFN_DIM ** -0.5),
        "final_norm": gain(ks[17], (D_MODEL,)),
    }


def reference(x, ffn1_norm, ffn1_w_in, ffn1_w_out, mix_norm, w_in, na_rpb, conv_w, conv_b,
              conv_ln_g, conv_ln_b, swa_sink, branch_norm, w_out, ffn2_norm, ffn2_w_in,
              ffn2_w_out, final_norm):
    slopes = alibi_slopes(SWA_HEADS + DIL_HEADS)
    sl_c = slopes[:SWA_HEADS]
    sl_d = slopes[SWA_HEADS:]
    h = x
    for l in range(DEPTH):
        h = h + 0.5 * swiglu(rms_norm(h, ffn1_norm[l]), ffn1_w_in[l], ffn1_w_out[l])
        h = h + token_mixing(rms_norm(h, mix_norm[l]), w_in[l], w_out[l], na_rpb[l], conv_w[l],
                             conv_b[l], conv_ln_g[l], conv_ln_b[l], swa_sink[l], branch_norm[l],
                             sl_c, sl_d)
        h = h + 0.5 * swiglu(rms_norm(h, ffn2_norm[l]), ffn2_w_in[l], ffn2_w_out[l])
    return rms_norm(h, final_norm)
```

```python
import numpy as np
from contextlib import ExitStack
import ml_dtypes
import concourse.bass as bass
import concourse.mybir as mybir
from concourse.bass_utils import run_bass_kernel_spmd

F32 = mybir.dt.float32
BF16 = mybir.dt.bfloat16
AF = mybir.ActivationFunctionType
ALU = mybir.AluOpType

S = 2048
D = 4096
FF = 6144
NCH = 32
TG = 1024
NTG = 2
UROWS = 9728
EPS = 1e-6
SCALE = float(128 ** -0.5)
LCOLS = 400
COL_FINAL = 1600
COL_SINK = 1632
NCOLS = 1664
TABW = 2944
TABOFF = 1408
N_CORES = 1
SLOPES = [float(2.0 ** (-8.0 * i / 16)) for i in range(1, 17)]

WKINDS = ["w1i", "w1o", "wmi", "wmo", "w2i", "w2o"]
WROWS = {"w1i": 96 * 128, "w1o": 64 * 128, "wmi": 76 * 128, "wmo": 32 * 128, "w2i": 96 * 128, "w2o": 64 * 128}
WWID = {"w1i": 4096, "w1o": 3072, "wmi": 4096, "wmo": 4096, "w2i": 4096, "w2o": 3072}


_UID = [0]


def _u(name):
    _UID[0] += 1
    return f"{name}_{_UID[0]}"


def _sbt(nc, name, shape, dt):
    return nc.sbuf_tensor(_u(name), list(shape), dt)


def _pst(nc, name, shape, dt):
    return nc.psum_tensor(_u(name), list(shape), dt)


_LIVE = []


class _SemCtx:
    def __init__(self, h):
        self.h = h

    def __enter__(self):
        return self.h

    def __exit__(self, *a):
        return False


def _sem(nc, name):
    h = nc.alloc_semaphore(name=_u(name))
    _LIVE.append(h)
    return _SemCtx(h)


class SlotSem:
    def __init__(self, nc, ps, name, n, per_item=1):
        self.sems = [ps.enter_context(_sem(nc, f"{name}{i}")) for i in range(n)]
        self.n = n
        self.per = per_item

    def sem(self, i):
        return self.sems[i % self.n]

    def val(self, i):
        return 16 * self.per * (i // self.n + 1)

    def wait(self, eng, i):
        eng.wait_ge(self.sem(i), self.val(i))


def _end_block(nc):
    def f():
        nc.all_engine_barrier()
        if _LIVE:
            nc.clear_and_free_semaphores(list(_LIVE))
            _LIVE.clear()
        nc.all_engine_barrier()
    return f


class Prog:
    pass


def build_program(depth, phases=("ffn1", "mix", "ffn2"), mix_parts=("A", "B", "C", "D"), nseq=1):
    nc = bass.Bass("TRN2", target_bir_lowering=False)
    P = Prog()
    P.nc = nc
    P.depth = depth
    L = depth
    dt_in = lambda name, shape, dt=F32: nc.dram_tensor(name, list(shape), dt, kind="ExternalInput").ap()
    x_all = dt_in("x", [nseq * S, D])
    out_all = nc.dram_tensor("out", [nseq * S, D], F32, kind="ExternalOutput").ap()
    P.w32 = {k: dt_in(k, [L * WROWS[k], WWID[k]]) for k in WKINDS}
    P.wbf = {k: [nc.dram_tensor(f"{k}_bf{l}", [WROWS[k], WWID[k]], BF16, kind="Internal").ap() for l in range(L)] for k in WKINDS}
    cols_d = dt_in("cols", [128, NCOLS])
    idf_d = dt_in("ident", [128, 128])
    dist_d = dt_in("dist", [128, TABW])
    maskc_d = dt_in("maskc", [128, TABW], BF16)
    maskd_d = dt_in("maskd", [128, TABW], BF16)
    maska_d = dt_in("maska", [128, 4 * 8 * 512], BF16)
    rtab_d = dt_in("rtab", [L * 8 * 128, 11 * 128])
    hT = nc.dram_tensor("hT", [D, S], F32, kind="Internal").ap()
    uT = nc.dram_tensor("uT", [UROWS, S], BF16, kind="Internal").ap()
    mT = nc.dram_tensor("mT", [D, S], BF16, kind="Internal").ap()
    ssd = nc.dram_tensor("ssd", [4 * 128, S], F32, kind="Internal").ap()
    P.hT, P.uT, P.mT, P.ssd = hT, uT, mT, ssd
    P.dist_d, P.maskc_d, P.maskd_d, P.maska_d, P.rtab_d = dist_d, maskc_d, maskd_d, maska_d, rtab_d

    es = ExitStack()
    with es:
        sb = lambda name, shape, dt: es.enter_context(_sbt(nc, name, list(shape), dt))
        cols = sb("cols", [128, NCOLS], F32)
        idf = sb("idf", [128, 128], F32)
        idb = sb("idb", [128, 128], BF16)
        onesf = sb("onesf", [128, 128], F32)
        onesb = sb("onesb", [128, 128], BF16)
        esink = sb("esink", [128, 32], F32)
        epsb = sb("epsb", [128, 1], F32)
        P.epsb = epsb
        P.cols, P.idf, P.idb, P.onesf, P.onesb, P.esink = cols, idf, idb, onesf, onesb, esink
        _LIVE.clear()
        scrub = [nc.alloc_semaphore(name=f"scrub{i}") for i in range(24)]
        nc.all_engine_barrier()
        nc.clear_and_free_semaphores(scrub)
        nc.all_engine_barrier()
        castsem = [nc.alloc_semaphore(name=f"cast{i}") for i in range(3)]
        P.castsem = castsem
        stages = [(l, k) for l in range(L) for k in WKINDS]
        P.stage_idx = {st: i for i, st in enumerate(stages)}
        cast_cum = [0, 0, 0]
        P.stage_target = {}
        P.stage_ndma = {}
        CCH = 2048
        for i, (l, k) in enumerate(stages):
            nd = -(-WROWS[k] // CCH)
            cast_cum[i % 3] += 16 * nd
            P.stage_target[i] = cast_cum[i % 3]
        P.issued = 0

        def issue_casts(g, upto):
            upto = min(upto, len(stages) - 1)
            while P.issued <= upto:
                i = P.issued
                l, k = stages[i]
                r0 = l * WROWS[k]
                for r in range(0, WROWS[k], CCH):
                    rr = min(CCH, WROWS[k] - r)
                    g.dma_start(out=P.wbf[k][l][r:r + rr, :], in_=P.w32[k][r0 + r:r0 + r + rr, :]).then_inc(castsem[i % 3], 16)
                P.issued += 1
        P.issue_casts = issue_casts

        def cast_wait(eng, l, k):
            i = P.stage_idx[(l, k)]
            eng.wait_ge(castsem[i % 3], P.stage_target[i])
        P.cast_wait = cast_wait

        for sq in range(nseq):
            x = x_all[sq * S:(sq + 1) * S, :]
            out = out_all[sq * S:(sq + 1) * S, :]
            first = sq == 0
            with ExitStack() as ps:
                xt = ps.enter_context(_sbt(nc, "xt", [128, 2, D], F32))
                hst = ps.enter_context(_sbt(nc, "hst", [128, 2, NCH, 128], F32))
                tp = [ps.enter_context(_pst(nc, f"tp{i}", [128, 4, 128], F32)) for i in range(4)]
                sld = SlotSem(nc, ps, "p0ld", 2)
                sc = ps.enter_context(_sem(nc, "p0c"))
                stp = ps.enter_context(_sem(nc, "p0tp"))
                scp = ps.enter_context(_sem(nc, "p0cp"))
                sst = SlotSem(nc, ps, "p0st", 2)
                ps.callback(_end_block(nc))
                blk = ps.enter_context(nc.Block())
                NT = S // 128

                @blk.gpsimd
                def _(g):
                    if first:
                        issue_casts(g, len(stages) - 1)
                        for i in range(len(stages)):
                            g.wait_ge(castsem[i % 3], P.stage_target[i])

                @blk.sync
                def _(s):
                    if first:
                        s.dma_start(out=cols[:], in_=cols_d).then_inc(sc, 16)
                        s.dma_start(out=idf[:], in_=idf_d).then_inc(sc, 16)
                    for tt in range(NT + 1):
                        if tt < NT:
                            if tt >= 2:
                                s.wait_ge(stp, 8 * (tt - 1))
                            s.dma_start(out=xt[:, tt % 2, :], in_=x[tt * 128:(tt + 1) * 128, :]).then_inc(sld.sem(tt), 16)
                        if tt >= 1:
                            s.wait_ge(scp, 8 * tt)
                            s.dma_start(out=hT.rearrange("(c p) t -> p c t", p=128)[:, :, (tt - 1) * 128:tt * 128],
                                        in_=hst[:, (tt - 1) % 2, :, :]).then_inc(sst.sem(tt - 1), 16)
                    sst.wait(s, NT - 2)
                    sst.wait(s, NT - 1)

                @blk.vector
                def _(v):
                    if first:
                        v.wait_ge(sc, 32)
                        v.tensor_copy(idb[:], idf[:])
                        v.memset(onesf[:], 1.0)
                        v.memset(onesb[:], 1.0)
                        v.memset(epsb[:], EPS)
                    for tt in range(NT):
                        for q in range(8):
                            k = tt * 8 + q
                            v.wait_ge(stp, k + 1)
                            if q == 0 and tt >= 2:
                                sst.wait(v, tt - 2)
                            v.tensor_copy(hst[:, tt % 2, q * 4:(q + 1) * 4, :], tp[k % 4][:]).then_inc(scp, 1)

                @blk.scalar
                def _(a):
                    if first:
                        a.wait_ge(sc, 32)
                        a.activation(out=esink[:, 0:8 * L], in_=cols[:, COL_SINK:COL_SINK + 8 * L], func=AF.Exp)

                @blk.tensor
                def _(t):
                    if first:
                        t.wait_ge(sc, 32)
                    for tt in range(NT):
                        sld.wait(t, tt)
                        for q in range(8):
                            k = tt * 8 + q
                            if k >= 4:
                                t.wait_ge(scp, k - 3)
                            for i in range(4):
                                c = q * 4 + i
                                ins = t.transpose(tp[k % 4][:, i, :], xt[:, tt % 2, c * 128:(c + 1) * 128], idf[:])
                            ins.then_inc(stp, 1)
            nc.all_engine_barrier()

            for l in range(L):
                if "ffn1" in phases:
                    ffn(P, l, 0)
                if "mix" in phases:
                    mixer(P, l, mix_parts)
                if "ffn2" in phases:
                    ffn(P, l, 1)

            final_phase(P, out)
    return nc


def norm_block(P, tg, colbase, xnT):
    nc = P.nc
    with ExitStack() as ps:
        hb = ps.enter_context(_sbt(nc, "nb_h", [128, NCH, 512], F32))
        sq = ps.enter_context(_sbt(nc, "nb_sq", [128, 2, 512], F32))
        rstd = ps.enter_context(_sbt(nc, "nb_rstd", [128, 512], F32))
        ssp = ps.enter_context(_pst(nc, "nb_ss", [128, 512], F32))
        ld = ps.enter_context(_sem(nc, "nb_ld"))
        sqr = ps.enter_context(_sem(nc, "nb_sqr"))
        sqf = ps.enter_context(_sem(nc, "nb_sqf"))
        vd = ps.enter_context(_sem(nc, "nb_vd"))
        rs = ps.enter_context(_sem(nc, "nb_rs"))
        ps.callback(_end_block(nc))
        blk = ps.enter_context(nc.Block())
        hv = P.hT.rearrange("(c p) t -> p c t", p=128)

        @blk.sync
        def _(s):
            for th in range(2):
                if th >= 1:
                    s.wait_ge(vd, th)
                t0 = tg * TG + th * 512
                for q in range(4):
                    s.dma_start(out=hb[:, q * 8:(q + 1) * 8, :], in_=hv[:, q * 8:(q + 1) * 8, t0:t0 + 512]).then_inc(ld, 16)

        @blk.scalar
        def _(a):
            for th in range(2):
                for c in range(NCH):
                    k = th * NCH + c
                    if c == 0:
                        a.wait_ge(ld, 64 * (th + 1))
                    if k >= 2:
                        a.wait_ge(sqf, k - 1)
                    a.activation(out=sq[:, k % 2, :], in_=hb[:, c, :], func=AF.Square).then_inc(sqr, 1)
                a.wait_ge(sqf, NCH * (th + 1))
                a.activation(out=rstd[:], in_=ssp[:], func=AF.Sqrt, bias=P.epsb[:], scale=1.0 / D).then_inc(rs, 1)

        @blk.tensor
        def _(t):
            for th in range(2):
                if th >= 1:
                    t.wait_ge(vd, th)
                for c in range(NCH):
                    k = th * NCH + c
                    t.wait_ge(sqr, k + 1)
                    t.matmul(ssp[:], lhsT=P.onesf[:], rhs=sq[:, k % 2, :], start=(c == 0), stop=(c == NCH - 1)).then_inc(sqf, 1)

        @blk.vector
        def _(v):
            for th in range(2):
                v.wait_ge(rs, th + 1)
                v.reciprocal(rstd[:], rstd[:])
                for c in range(NCH):
                    ins = v.scalar_tensor_tensor(out=xnT[:, c, th * 512:(th + 1) * 512], in0=hb[:, c, :],
                                                 scalar=P.cols[:, colbase + c:colbase + c + 1], in1=rstd[:],
                                                 op0=ALU.mult, op1=ALU.mult)
                ins.then_inc(vd, 1)
    nc.all_engine_barrier()


def emit_wload(P, s, wsrc, piece0, NP, KC, wring, wld, pready, l, kind):
    NW = len(wring)
    P.cast_wait(s, l, kind)
    wsrc = P.wbf[kind][l]
    piece0 = piece0 - l * (WROWS[kind] // 128)
    for p in range(NP):
        if p >= NW:
            s.wait_ge(pready, p - NW + 1)
        r = (piece0 + p) * 128
        s.dma_start(out=wring[p % NW][:, 0:KC * 128], in_=wsrc[r:r + 128, :]).then_inc(wld.sem(p), 16)


def emit_mm(t, NP, KC, inT, wring, wld, psl, pready, pfree_fn):
    NW = len(wring)
    NPB = len(psl)
    for p in range(NP):
        wld.wait(t, p)
        pfree_fn(t, p)
        w = wring[p % NW]
        for kc in range(KC):
            for th in range(2):
                ins = t.matmul(psl[p % NPB][th][:], lhsT=w[:, kc * 128:(kc + 1) * 128],
                               rhs=inT[:, kc, th * 512:(th + 1) * 512], start=(kc == 0), stop=(kc == KC - 1))
        ins.then_inc(pready, 1)


def gemm_resid_block(P, tg, inT, KC, l, kind, piece0, alpha):
    nc = P.nc
    NP = NCH
    NR = 3
    with ExitStack() as ps:
        wring = [ps.enter_context(_sbt(nc, f"gr_w{i}", [128, 4096], BF16)) for i in range(3)]
        rr = ps.enter_context(_sbt(nc, "gr_r", [128, NR, TG], F32))
        psl = [[ps.enter_context(_pst(nc, f"gr_ps{i}_{th}", [128, 512], F32)) for th in range(2)] for i in range(3)]
        wld = SlotSem(nc, ps, "gr_wld", 3)
        pready = ps.enter_context(_sem(nc, "gr_pr"))
        rld = SlotSem(nc, ps, "gr_rld", 3)
        vd = ps.enter_context(_sem(nc, "gr_vd"))
        st = SlotSem(nc, ps, "gr_st", 3)
        ps.callback(_end_block(nc))
        blk = ps.enter_context(nc.Block())
        t0 = tg * TG

        @blk.sync
        def _(s):
            emit_wload(P, s, P.wbf[kind], piece0, NP, KC, wring, wld, pready, l, kind)

        @blk.tensor
        def _(t):
            def pfree(t, p):
                if p >= 3:
                    t.wait_ge(vd, p - 2)
            emit_mm(t, NP, KC, inT, wring, wld, psl, pready, pfree)

        @blk.scalar
        def _(a):
            for c in range(NP + 1):
                if c < NP:
                    if c >= NR:
                        st.wait(a, c - NR)
                    a.dma_start(out=rr[:, c % NR, :], in_=P.hT[c * 128:(c + 1) * 128, t0:t0 + TG]).then_inc(rld.sem(c), 16)
                if c >= 1:
                    a.wait_ge(vd, c)
                    a.dma_start(out=P.hT[(c - 1) * 128:c * 128, t0:t0 + TG], in_=rr[:, (c - 1) % NR, :]).then_inc(st.sem(c - 1), 16)
            for c in range(NP - NR, NP):
                st.wait(a, c)

        @blk.vector
        def _(v):
            for c in range(NP):
                v.wait_ge(pready, c + 1)
                rld.wait(v, c)
                for th in range(2):
                    ins = v.scalar_tensor_tensor(out=rr[:, c % NR, th * 512:(th + 1) * 512], in0=psl[c % 3][th][:],
                                                 scalar=alpha, in1=rr[:, c % NR, th * 512:(th + 1) * 512],
                                                 op0=ALU.mult, op1=ALU.add)
                ins.then_inc(vd, 1)
    nc.all_engine_barrier()


def ffn(P, l, which):
    nc = P.nc
    ki = "w1i" if which == 0 else "w2i"
    ko = "w1o" if which == 0 else "w2o"
    colbase = l * LCOLS + (0 if which == 0 else 64)
    with ExitStack() as outer:
        xnT = outer.enter_context(_sbt(nc, "ffn_xn", [128, NCH, TG], BF16))
        for tg in range(NTG):
            norm_block(P, tg, colbase, xnT)
            with ExitStack() as mid:
                h1T = mid.enter_context(_sbt(nc, "ffn_h1", [128, 24, TG], BF16))
                for hf in range(2):
                    with ExitStack() as ps:
                        wring = [ps.enter_context(_sbt(nc, f"g1_w{i}", [128, 4096], BF16)) for i in range(3)]
                        tmp = ps.enter_context(_sbt(nc, "g1_tmp", [128, 2, TG], F32))
                        psl = [[ps.enter_context(_pst(nc, f"g1_ps{i}_{th}", [128, 512], F32)) for th in range(2)] for i in range(4)]
                        wld = SlotSem(nc, ps, "g1_wld", 3)
                        pready = ps.enter_context(_sem(nc, "g1_pr"))
                        fa = ps.enter_context(_sem(nc, "g1_fa"))
                        fv = ps.enter_context(_sem(nc, "g1_fv"))
                        ps.callback(_end_block(nc))
                        blk = ps.enter_context(nc.Block())
                        NP = 48
                        piece0 = l * 96 + hf * 48

                        @blk.sync
                        def _(s):
                            emit_wload(P, s, P.wbf[ki], piece0, NP, NCH, wring, wld, pready, l, ki)

                        @blk.tensor
                        def _(t):
                            def pfree(t, p):
                                if p >= 4:
                                    q = p - 4
                                    t.wait_ge(fa if q % 2 == 0 else fv, q // 2 + 1)
                            emit_mm(t, NP, NCH, xnT, wring, wld, psl, pready, pfree)

                        @blk.scalar
                        def _(a):
                            for jl in range(24):
                                a.wait_ge(pready, 2 * jl + 1)
                                if jl >= 2:
                                    a.wait_ge(fv, jl - 1)
                                for th in range(2):
                                    ins = a.activation(out=tmp[:, jl % 2, th * 512:(th + 1) * 512],
                                                       in_=psl[(2 * jl) % 4][th][:], func=AF.Silu)
                                ins.then_inc(fa, 1)

                        @blk.vector
                        def _(v):
                            for jl in range(24):
                                v.wait_ge(pready, 2 * jl + 2)
                                v.wait_ge(fa, jl + 1)
                                for th in range(2):
                                    ins = v.tensor_tensor(out=h1T[:, jl, th * 512:(th + 1) * 512],
                                                          in0=tmp[:, jl % 2, th * 512:(th + 1) * 512],
                                                          in1=psl[(2 * jl + 1) % 4][th][:], op=ALU.mult)
                                ins.then_inc(fv, 1)
                    nc.all_engine_barrier()
                    gemm_resid_block(P, tg, h1T, 24, l, ko, l * 64 + hf * 32, 0.5)


def inproj_block(P, tg, xnT, l):
    nc = P.nc
    NP = 76
    with ExitStack() as ps:
        wring = [ps.enter_context(_sbt(nc, f"ip_w{i}", [128, 4096], BF16)) for i in range(3)]
        stg = ps.enter_context(_sbt(nc, "ip_stg", [128, 3, TG], BF16))
        psl = [[ps.enter_context(_pst(nc, f"ip_ps{i}_{th}", [128, 512], F32)) for th in range(2)] for i in range(3)]
        wld = SlotSem(nc, ps, "ip_wld", 3)
        pready = ps.enter_context(_sem(nc, "ip_pr"))
        ca = ps.enter_context(_sem(nc, "ip_ca"))
        cv = ps.enter_context(_sem(nc, "ip_cv"))
        ust = SlotSem(nc, ps, "ip_ust", 3)
        ps.callback(_end_block(nc))
        blk = ps.enter_context(nc.Block())
        t0 = tg * TG

        @blk.sync
        def _(s):
            wsrc = P.wbf["wmi"][l]
            for p in range(NP + 2):
                if p < NP:
                    if p >= 3:
                        s.wait_ge(pready, p - 2)
                    s.dma_start(out=wring[p % 3][:, 0:NCH * 128], in_=wsrc[p * 128:(p + 1) * 128, :]).then_inc(wld.sem(p), 16)
                n = p - 2
                if n >= 0:
                    s.wait_ge(ca, n + 1)
                    s.wait_ge(cv, n + 1)
                    s.dma_start(out=P.uT[n * 128:(n + 1) * 128, t0:t0 + TG], in_=stg[:, n % 3, :]).then_inc(ust.sem(n), 16)
            for n in range(NP - 3, NP):
                ust.wait(s, n)

        @blk.tensor
        def _(t):
            def pfree(t, p):
                if p >= 3:
                    t.wait_ge(ca, p - 2)
                    t.wait_ge(cv, p - 2)
            emit_mm(t, NP, NCH, xnT, wring, wld, psl, pready, pfree)

        @blk.scalar
        def _(a):
            for n in range(NP):
                a.wait_ge(pready, n + 1)
                if n >= 3:
                    ust.wait(a, n - 3)
                a.copy(stg[:, n % 3, 0:512], psl[n % 3][0][:]).then_inc(ca, 1)

        @blk.vector
        def _(v):
            for n in range(NP):
                v.wait_ge(pready, n + 1)
                if n >= 3:
                    ust.wait(v, n - 3)
                v.tensor_copy(stg[:, n % 3, 512:1024], psl[n % 3][1][:]).then_inc(cv, 1)
    nc.all_engine_barrier()


ATT = {
    "A": dict(q=0, k=1024, v=2048, nkv=8, m=0, ss=0),
    "C": dict(q=5120, k=6144, v=6400, nkv=2, m=2048, ss=2),
    "D": dict(q=6656, k=7680, v=8704, nkv=8, m=3072, ss=3),
}


def att_keytiles(kind, g):
    if kind == "A":
        return list(range(NA_J0[g], NA_J0[g] + NA_NJ[g]))
    if kind == "C":
        return list(range(max(0, 4 * g - 1), min(16, 4 * g + 5)))
    return list(range(max(0, 4 * g - 8), min(16, 4 * g + 12)))


def attn_block(P, l, kind):
    nc = P.nc
    cfg = ATT[kind]
    isA = kind == "A"
    units = []
    for g in range(4):
        js = att_keytiles(kind, g)
        for j in js:
            units.append((g, j, j == js[0], j == js[-1]))
    U = len(units)
    LDH = 64 if isA else 48
    TBLN = 128 if isA else 32
    with ExitStack() as ps:
        qb = ps.enter_context(_sbt(nc, "at_q", [128, 2, S], BF16))
        kb = ps.enter_context(_sbt(nc, "at_k", [128, 2, S], BF16))
        vb = ps.enter_context(_sbt(nc, "at_v", [128, 2, S], BF16))
        V = ps.enter_context(_sbt(nc, "at_V", [128, 2, 16, 128], BF16))
        if isA:
            rt = ps.enter_context(_sbt(nc, "at_rt", [128, 2, 11 * 128], F32))
            mk = ps.enter_context(_sbt(nc, "at_ma", [128, 4 * 8 * 512], BF16))
        else:
            dist = ps.enter_context(_sbt(nc, "at_dist", [128, TABW], F32))
            mk = ps.enter_context(_sbt(nc, "at_mk", [128, TABW], BF16))
        tmp = ps.enter_context(_sbt(nc, "at_tmp", [128, 3, 512], F32))
        pr = ps.enter_context(_sbt(nc, "at_pr", [128, 3, 512], F32))
        pp = ps.enter_context(_sbt(nc, "at_pp", [128, 3, 512], BF16))
        den = ps.enter_context(_sbt(nc, "at_den", [128, 512], F32))
        osb = ps.enter_context(_sbt(nc, "at_o", [128, 2, 512], F32))
        osq = ps.enter_context(_sbt(nc, "at_osq", [128, 512], BF16))
        stage = ps.enter_context(_sbt(nc, "at_stage", [128, 2, S], BF16))
        ssacc = ps.enter_context(_sbt(nc, "at_ssacc", [128, S], F32))
        s_ps = [ps.enter_context(_pst(nc, f"at_s{i}", [128, 512], F32)) for i in range(2)]
        o_ps = [ps.enter_context(_pst(nc, f"at_ops{i}", [128, 512], F32)) for i in range(2)]
        d_ps = [ps.enter_context(_pst(nc, f"at_dps{i}", [128, 512], F32)) for i in range(2)]
        ss_ps = ps.enter_context(_pst(nc, "at_ssps", [128, 512], F32))
        tp_ps = ps.enter_context(_pst(nc, "at_tp", [128, 8, 128], BF16))
        sem = lambda n: ps.enter_context(_sem(nc, "at_" + n))
        tbl, vtp, vcp, sr, tf, er, ppr, pvd, od, sqd, ssr, ssf, sso = [sem(n) for n in
            ("tbl", "vtp", "vcp", "sr", "tf", "er", "ppr", "pvd", "od", "sqd", "ssr", "ssf", "sso")]
        ld = SlotSem(nc, ps, "at_ld", 2, per_item=LDH // 16)
        mst = SlotSem(nc, ps, "at_mst", 2)
        ps.callback(_end_block(nc))
        blk = ps.enter_context(nc.Block())
        NH = 8

        def mask_ap(g, j):
            if isA:
                jj = j - NA_J0[g]
                o = (g * 8 + jj) * 512
            else:
                o = 512 * g - 128 * j + TABOFF
            return mk[:, o:o + 512]

        @blk.sync
        def _(s):
            if isA:
                for q4 in range(8):
                    s.dma_start(out=mk[:, q4 * 2048:(q4 + 1) * 2048], in_=P.maska_d[:, q4 * 2048:(q4 + 1) * 2048]).then_inc(tbl, 16)
            else:
                s.dma_start(out=dist[:], in_=P.dist_d).then_inc(tbl, 16)
                s.dma_start(out=mk[:], in_=(P.maskc_d if kind == "C" else P.maskd_d)).then_inc(tbl, 16)
            s.wait_ge(tbl, TBLN)
            for hh in range(NH):
                if hh >= 2:
                    s.wait_ge(pvd, (hh - 1) * U)
                kvh = hh if cfg["nkv"] == 8 else hh // 4
                s.dma_start(out=qb[:, hh % 2, :], in_=P.uT[cfg["q"] + hh * 128:cfg["q"] + (hh + 1) * 128, :]).then_inc(ld.sem(hh), 16)
                s.dma_start(out=kb[:, hh % 2, :], in_=P.uT[cfg["k"] + kvh * 128:cfg["k"] + (kvh + 1) * 128, :]).then_inc(ld.sem(hh), 16)
                s.dma_start(out=vb[:, hh % 2, :], in_=P.uT[cfg["v"] + kvh * 128:cfg["v"] + (kvh + 1) * 128, :]).then_inc(ld.sem(hh), 16)
                if isA:
                    r0 = (l * 8 + hh) * 128
                    s.dma_start(out=rt[:, hh % 2, :], in_=P.rtab_d[r0:r0 + 128, :]).then_inc(ld.sem(hh), 16)
                if hh >= 1:
                    s.wait_ge(sqd, 4 * hh)
                    s.dma_start(out=P.mT[cfg["m"] + (hh - 1) * 128:cfg["m"] + hh * 128, :], in_=stage[:, (hh - 1) % 2, :]).then_inc(mst.sem(hh - 1), 16)
            s.wait_ge(sqd, 4 * NH)
            s.dma_start(out=P.mT[cfg["m"] + (NH - 1) * 128:cfg["m"] + NH * 128, :], in_=stage[:, (NH - 1) % 2, :]).then_inc(mst.sem(NH - 1), 16)
            s.wait_ge(ssf, 4 * NH)
            s.dma_start(out=P.ssd[cfg["ss"] * 128:(cfg["ss"] + 1) * 128, :], in_=ssacc[:, :]).then_inc(sso, 16)
            mst.wait(s, NH - 2)
            mst.wait(s, NH - 1)
            s.wait_ge(sso, 16)

        @blk.gpsimd
        def _(gp):
            gp.wait_ge(tbl, TBLN)
            for hh in range(NH):
                for u, (g, j, first, last) in enumerate(units):
                    gu = hh * U + u
                    gp.wait_ge(er, gu + 1)
                    if gu >= 3:
                        gp.wait_ge(pvd, gu - 2)
                    gp.tensor_tensor(out=pp[:, gu % 3, :], in0=pr[:, gu % 3, :], in1=mask_ap(g, j), op=ALU.mult).then_inc(ppr, 1)

        @blk.tensor
        def _(t):
            def ssmm(G):
                t.wait_ge(sqd, G + 1)
                if G >= 1:
                    t.wait_ge(ssf, G)
                t.matmul(ss_ps[:], lhsT=P.onesb[:], rhs=osq[:], start=True, stop=True).then_inc(ssr, 1)
            for hh in range(NH):
                ld.wait(t, hh)
                for b in range(2):
                    k = 2 * hh + b
                    if k >= 1:
                        t.wait_ge(vcp, k)
                    for i in range(8):
                        ins = t.transpose(tp_ps[:, i, :], vb[:, hh % 2, (8 * b + i) * 128:(8 * b + i + 1) * 128], P.idb[:])
                    ins.then_inc(vtp, 1)
                t.wait_ge(vcp, 2 * (hh + 1))
                for uu in range(U + 1):
                    if uu < U:
                        g, j, first, last = units[uu]
                        gu = hh * U + uu
                        if gu >= 2:
                            t.wait_ge(tf, gu - 1)
                        t.matmul(s_ps[gu % 2][:], lhsT=kb[:, hh % 2, j * 128:(j + 1) * 128],
                                 rhs=qb[:, hh % 2, g * 512:(g + 1) * 512], start=True, stop=True).then_inc(sr, 1)
                    if uu >= 1:
                        g, j, first, last = units[uu - 1]
                        gu = hh * U + uu - 1
                        G = hh * 4 + g
                        t.wait_ge(ppr, gu + 1)
                        if first and G >= 2:
                            t.wait_ge(od, G - 1)
                        t.matmul(o_ps[G % 2][:], lhsT=V[:, hh % 2, j, :], rhs=pp[:, gu % 3, :], start=first, stop=last)
                        t.matmul(d_ps[G % 2][:], lhsT=P.onesb[:], rhs=pp[:, gu % 3, :], start=first, stop=last).then_inc(pvd, 1)
                        if last and G >= 1:
                            ssmm(G - 1)
            ssmm(4 * NH - 1)

        @blk.vector
        def _(v):
            v.wait_ge(tbl, TBLN)

            def post(hh, g, gu_last):
                G = hh * 4 + g
                v.wait_ge(pvd, gu_last + 1)
                if G >= 2:
                    v.wait_ge(sqd, G - 1)
                if kind == "C":
                    v.tensor_scalar(out=den[:], in0=d_ps[G % 2][:], scalar1=P.esink[:, l * 8 + hh:l * 8 + hh + 1], scalar2=None, op0=ALU.add)
                else:
                    v.tensor_copy(den[:], d_ps[G % 2][:])
                v.reciprocal(den[:], den[:])
                v.tensor_tensor(out=osb[:, G % 2, :], in0=o_ps[G % 2][:], in1=den[:], op=ALU.mult).then_inc(od, 1)

            def addss(G):
                hh, g = G // 4, G % 4
                v.wait_ge(ssr, G + 1)
                if hh == 0:
                    v.tensor_copy(ssacc[:, g * 512:(g + 1) * 512], ss_ps[:]).then_inc(ssf, 1)
                else:
                    v.tensor_tensor(out=ssacc[:, g * 512:(g + 1) * 512], in0=ssacc[:, g * 512:(g + 1) * 512], in1=ss_ps[:], op=ALU.add).then_inc(ssf, 1)

            for hh in range(NH):
                for b in range(2):
                    k = 2 * hh + b
                    v.wait_ge(vtp, k + 1)
                    v.tensor_copy(V[:, hh % 2, 8 * b:8 * b + 8, :], tp_ps[:]).then_inc(vcp, 1)
                if isA:
                    ld.wait(v, hh)
                pend = None
                for u, (g, j, first, last) in enumerate(units):
                    gu = hh * U + u
                    v.wait_ge(sr, gu + 1)
                    if gu >= 3:
                        v.wait_ge(er, gu - 2)
                    if isA:
                        i0 = 5 - (j - 4 * g)
                        v.scalar_tensor_tensor(out=tmp[:, gu % 3, :], in0=s_ps[gu % 2][:], scalar=SCALE,
                                               in1=rt[:, hh % 2, i0 * 128:i0 * 128 + 512], op0=ALU.mult, op1=ALU.add).then_inc(tf, 1)
                    else:
                        o = 512 * g - 128 * j + TABOFF
                        slope = SLOPES[hh] if kind == "C" else SLOPES[8 + hh]
                        v.scalar_tensor_tensor(out=tmp[:, gu % 3, :], in0=dist[:, o:o + 512], scalar=-slope / SCALE,
                                               in1=s_ps[gu % 2][:], op0=ALU.mult, op1=ALU.add).then_inc(tf, 1)
                    if pend is not None:
                        post(*pend)
                        Gp = pend[0] * 4 + pend[1]
                        if Gp >= 1:
                            addss(Gp - 1)
                        pend = None
                    if last:
                        pend = (hh, g, gu)
                post(*pend)
                Gp = pend[0] * 4 + pend[1]
                if Gp >= 1:
                    addss(Gp - 1)
            addss(4 * NH - 1)

        @blk.scalar
        def _(a):
            def post(hh, g):
                G = hh * 4 + g
                a.wait_ge(od, G + 1)
                if hh >= 2 and g == 0:
                    mst.wait(a, hh - 2)
                a.copy(stage[:, hh % 2, g * 512:(g + 1) * 512], osb[:, G % 2, :])
                if G >= 1:
                    a.wait_ge(ssr, G)
                a.activation(out=osq[:], in_=osb[:, G % 2, :], func=AF.Square).then_inc(sqd, 1)
            for hh in range(NH):
                pend = None
                cnt = 0
                for u, (g, j, first, last) in enumerate(units):
                    gu = hh * U + u
                    a.wait_ge(tf, gu + 1)
                    if gu >= 3:
                        a.wait_ge(ppr, gu - 2)
                    a.activation(out=pr[:, gu % 3, :], in_=tmp[:, gu % 3, :], func=AF.Exp,
                                 scale=(1.0 if isA else SCALE)).then_inc(er, 1)
                    if pend is not None:
                        cnt += 1
                        if cnt >= 2:
                            post(*pend)
                            pend = None
                    if last:
                        if pend is not None:
                            post(*pend)
                        pend = (hh, g)
                        cnt = 0
                post(*pend)
    nc.all_engine_barrier()


def conv_block(P, l):
    nc = P.nc
    cb = l * LCOLS
    with ExitStack() as ps:
        ab = ps.enter_context(_sbt(nc, "cv_a", [128, 2, S], BF16))
        gb = ps.enter_context(_sbt(nc, "cv_g", [128, 2, S], BF16))
        sg = ps.enter_context(_sbt(nc, "cv_sg", [128, 2, S], F32))
        hc = ps.enter_context(_sbt(nc, "cv_hc", [128, 2, S + 32], F32))
        cv = ps.enter_context(_sbt(nc, "cv_cv", [128, 8, S], F32))
        stage = ps.enter_context(_sbt(nc, "cv_stage", [128, 8, S], BF16))
        sqt = ps.enter_context(_sbt(nc, "cv_sqt", [128, 2, 512], F32))
        mean = ps.enter_context(_sbt(nc, "cv_mean", [128, 512], F32))
        msq = ps.enter_context(_sbt(nc, "cv_msq", [128, 512], F32))
        rstd = ps.enter_context(_sbt(nc, "cv_rstd", [128, 512], F32))
        t1 = ps.enter_context(_sbt(nc, "cv_t1", [128, 2, 512], F32))
        ob = ps.enter_context(_sbt(nc, "cv_ob", [128, 2, 512], F32))
        osq = ps.enter_context(_sbt(nc, "cv_osq", [128, 2, 512], F32))
        ssacc = ps.enter_context(_sbt(nc, "cv_ssacc", [128, S], F32))
        s1_ps = ps.enter_context(_pst(nc, "cv_s1", [128, 512], F32))
        s2_ps = ps.enter_context(_pst(nc, "cv_s2", [128, 512], F32))
        ss_ps = ps.enter_context(_pst(nc, "cv_ss", [128, 512], F32))
        sem = lambda n: ps.enter_context(_sem(nc, "cv_" + n))
        sig, cvdV, cvdP, sqr, sqf, vmr, rsr, t1r, obf, ssm, ssc, mst = [sem(n) for n in
            ("sig", "cvdV", "cvdP", "sqr", "sqf", "vmr", "rsr", "t1r", "obf", "ssm", "ssc", "mst")]
        ld = SlotSem(nc, ps, "cv_ld", 2, per_item=2)
        ps.callback(_end_block(nc))
        blk = ps.enter_context(nc.Block())
        AROW, GROW = 3072, 4096

        @blk.sync
        def _(s):
            for c in range(8):
                if c >= 2:
                    s.wait_ge(cvdV if c % 2 == 0 else cvdP, c // 2)
                s.dma_start(out=ab[:, c % 2, :], in_=P.uT[AROW + c * 128:AROW + (c + 1) * 128, :]).then_inc(ld.sem(c), 16)
                s.dma_start(out=gb[:, c % 2, :], in_=P.uT[GROW + c * 128:GROW + (c + 1) * 128, :]).then_inc(ld.sem(c), 16)
            s.wait_ge(obf, 32)
            for c in range(8):
                s.dma_start(out=P.mT[1024 + c * 128:1024 + (c + 1) * 128, :], in_=stage[:, c, :]).then_inc(mst, 16)
            s.wait_ge(ssc, 4)
            s.dma_start(out=P.ssd[128:256, :], in_=ssacc[:, :]).then_inc(mst, 16)
            s.wait_ge(mst, 16 * 9)

        def conv_chunk(e, c, done):
            e.wait_ge(sig, c + 1)
            e.tensor_tensor(out=hc[:, c % 2, 15:15 + S], in0=ab[:, c % 2, :], in1=sg[:, c % 2, :], op=ALU.mult)
            wc = cb + 152 + c * 31
            e.tensor_scalar(out=cv[:, c, :], in0=hc[:, c % 2, 0:S], scalar1=P.cols[:, wc:wc + 1],
                            scalar2=P.cols[:, cb + 128 + c:cb + 129 + c], op0=ALU.mult, op1=ALU.add)
            for tau in range(1, 31):
                ins = e.scalar_tensor_tensor(out=cv[:, c, :], in0=hc[:, c % 2, tau:tau + S], scalar=P.cols[:, wc + tau:wc + tau + 1],
                                             in1=cv[:, c, :], op0=ALU.mult, op1=ALU.add)
            ins.then_inc(done, 1)


        @blk.scalar
        def _(a):
            for c in range(8):
                ld.wait(a, c)
                if c >= 2:
                    a.wait_ge(cvdV if c % 2 == 0 else cvdP, c // 2)
                a.activation(out=sg[:, c % 2, :], in_=gb[:, c % 2, :], func=AF.Sigmoid).then_inc(sig, 1)
            a.wait_ge(cvdV, 4)
            a.wait_ge(cvdP, 4)
            for b in range(4):
                bs = slice(b * 512, (b + 1) * 512)
                for c in range(8):
                    k = b * 8 + c
                    if k >= 2:
                        a.wait_ge(sqf, k - 1)
                    a.activation(out=sqt[:, k % 2, :], in_=cv[:, c, bs], func=AF.Square).then_inc(sqr, 1)
                a.wait_ge(vmr, b + 1)
                a.activation(out=rstd[:], in_=rstd[:], func=AF.Sqrt, bias=P.epsb[:], scale=1.0).then_inc(rsr, 1)
                for c in range(8):
                    k = b * 8 + c
                    a.wait_ge(t1r, k + 1)
                    if k >= 2:
                        a.wait_ge(ssm, k - 1)
                    a.activation(out=ob[:, k % 2, :], in_=t1[:, k % 2, :], func=AF.Silu,
                                 bias=P.cols[:, cb + 144 + c:cb + 145 + c], scale=P.cols[:, cb + 136 + c:cb + 137 + c])
                    a.copy(stage[:, c, bs], ob[:, k % 2, :])
                    a.activation(out=osq[:, k % 2, :], in_=ob[:, k % 2, :], func=AF.Square).then_inc(obf, 1)

        @blk.tensor
        def _(t):
            for b in range(4):
                bs = slice(b * 512, (b + 1) * 512)
                if b >= 1:
                    t.wait_ge(vmr, b)
                for c in range(8):
                    k = b * 8 + c
                    t.wait_ge(sqr, k + 1)
                    t.matmul(s1_ps[:], lhsT=P.onesf[:], rhs=cv[:, c, bs], start=(c == 0), stop=(c == 7))
                    t.matmul(s2_ps[:], lhsT=P.onesf[:], rhs=sqt[:, k % 2, :], start=(c == 0), stop=(c == 7)).then_inc(sqf, 1)
                if b >= 1:
                    t.wait_ge(ssc, b)
                for c in range(8):
                    k = b * 8 + c
                    t.wait_ge(obf, k + 1)
                    t.matmul(ss_ps[:], lhsT=P.onesf[:], rhs=osq[:, k % 2, :], start=(c == 0), stop=(c == 7)).then_inc(ssm, 1)

        @blk.vector
        def _(v):
            v.memset(hc[:, 0, :], 0.0)
            v.memset(hc[:, 1, :], 0.0)
            for c in range(8):
                conv_chunk(v, c, cvdV if c % 2 == 0 else cvdP)
            v.wait_ge(cvdP, 4)
            for b in range(4):
                bs = slice(b * 512, (b + 1) * 512)
                v.wait_ge(sqf, 8 * (b + 1))
                v.tensor_scalar(out=mean[:], in0=s1_ps[:], scalar1=1.0 / 1024, scalar2=None, op0=ALU.mult)
                v.tensor_tensor(out=msq[:], in0=mean[:], in1=mean[:], op=ALU.mult)
                v.scalar_tensor_tensor(out=rstd[:], in0=s2_ps[:], scalar=1.0 / 1024, in1=msq[:], op0=ALU.mult, op1=ALU.subtract).then_inc(vmr, 1)
                v.wait_ge(rsr, b + 1)
                v.reciprocal(rstd[:], rstd[:])
                for c in range(8):
                    k = b * 8 + c
                    if k >= 2:
                        v.wait_ge(obf, k - 1)
                    v.tensor_tensor(out=t1[:, k % 2, :], in0=cv[:, c, bs], in1=mean[:], op=ALU.subtract)
                    v.tensor_tensor(out=t1[:, k % 2, :], in0=t1[:, k % 2, :], in1=rstd[:], op=ALU.mult).then_inc(t1r, 1)
                v.wait_ge(ssm, 8 * (b + 1))
                v.tensor_copy(ssacc[:, bs], ss_ps[:]).then_inc(ssc, 1)
    nc.all_engine_barrier()


def outproj_prologue(P, tg, l, xin):
    nc = P.nc
    cb = l * LCOLS + 96
    with ExitStack() as ps:
        ssb = ps.enter_context(_sbt(nc, "op_ss", [128, 4, TG], F32))
        mr = ps.enter_context(_sbt(nc, "op_m", [128, 4, TG], BF16))
        ld = ps.enter_context(_sem(nc, "op_ld"))
        ml = SlotSem(nc, ps, "op_ml", 4)
        sq = ps.enter_context(_sem(nc, "op_sq"))
        vd = ps.enter_context(_sem(nc, "op_vd"))
        ps.callback(_end_block(nc))
        blk = ps.enter_context(nc.Block())
        t0 = tg * TG

        @blk.sync
        def _(s):
            for gi in range(4):
                s.dma_start(out=ssb[:, gi, :], in_=P.ssd[gi * 128:(gi + 1) * 128, t0:t0 + TG]).then_inc(ld, 16)
            for kc in range(NCH):
                if kc >= 4:
                    s.wait_ge(vd, kc - 3)
                s.dma_start(out=mr[:, kc % 4, :], in_=P.mT[kc * 128:(kc + 1) * 128, t0:t0 + TG]).then_inc(ml.sem(kc), 16)

        @blk.scalar
        def _(a):
            a.wait_ge(ld, 64)
            for gi in range(4):
                ins = a.activation(out=ssb[:, gi, :], in_=ssb[:, gi, :], func=AF.Sqrt, bias=P.epsb[:], scale=1.0 / 1024)
            ins.then_inc(sq, 1)

        @blk.vector
        def _(v):
            v.wait_ge(sq, 1)
            for gi in range(4):
                v.reciprocal(ssb[:, gi, :], ssb[:, gi, :])
            for kc in range(NCH):
                ml.wait(v, kc)
                v.scalar_tensor_tensor(out=xin[:, kc, :], in0=mr[:, kc % 4, :], scalar=P.cols[:, cb + kc:cb + kc + 1],
                                       in1=ssb[:, kc // 8, :], op0=ALU.mult, op1=ALU.mult).then_inc(vd, 1)
    nc.all_engine_barrier()


def mixer(P, l, parts):
    nc = P.nc
    with ExitStack() as outer:
        xnT = outer.enter_context(_sbt(nc, "mx_xn", [128, NCH, TG], BF16))
        for tg in range(NTG):
            norm_block(P, tg, l * LCOLS + 32, xnT)
            inproj_block(P, tg, xnT, l)
    for kind in ("A", "C", "D"):
        if kind in parts:
            attn_block(P, l, kind)
    if "B" in parts:
        conv_block(P, l)
    import os
    stop = os.environ.get("K_STOP", "")
    if stop == "noout":
        return
    with ExitStack() as outer:
        xin = outer.enter_context(_sbt(nc, "mx_xin", [128, NCH, TG], BF16))
        for tg in range(NTG):
            outproj_prologue(P, tg, l, xin)
            if stop == "nogemm":
                continue
            gemm_resid_block(P, tg, xin, NCH, l, "wmo", l * 32, 1.0)


def final_phase(P, out):
    nc = P.nc
    NT = S // 128
    with ExitStack() as ps:
        hb = ps.enter_context(_sbt(nc, "fp_h", [128, NCH, 512], F32))
        sq = ps.enter_context(_sbt(nc, "fp_sq", [128, 2, 512], F32))
        rstd = ps.enter_context(_sbt(nc, "fp_rstd", [128, 512], F32))
        ot = ps.enter_context(_sbt(nc, "fp_o", [128, 2, D], F32))
        ssp = ps.enter_context(_pst(nc, "fp_ss", [128, 512], F32))
        tp = [ps.enter_context(_pst(nc, f"fp_tp{i}", [128, 4, 128], F32)) for i in range(4)]
        ld = ps.enter_context(_sem(nc, "fp_ld"))
        sqr = ps.enter_context(_sem(nc, "fp_sqr"))
        sqf = ps.enter_context(_sem(nc, "fp_sqf"))
        vd = ps.enter_context(_sem(nc, "fp_vd"))
        stp = ps.enter_context(_sem(nc, "fp_tp"))
        scp = ps.enter_context(_sem(nc, "fp_cp"))
        sst = SlotSem(nc, ps, "fp_st", 2)
        rs = ps.enter_context(_sem(nc, "fp_rs"))
        ps.callback(_end_block(nc))
        blk = ps.enter_context(nc.Block())
        hv = P.hT.rearrange("(c p) t -> p c t", p=128)
        NB = S // 512

        @blk.sync
        def _(s):
            for b in range(NB):
                if b >= 1:
                    s.wait_ge(scp, 32 * b)
                for q in range(4):
                    s.dma_start(out=hb[:, q * 8:(q + 1) * 8, :], in_=hv[:, q * 8:(q + 1) * 8, b * 512:(b + 1) * 512]).then_inc(ld, 16)
                for t4 in range(4):
                    tt = b * 4 + t4
                    s.wait_ge(scp, 8 * (tt + 1))
                    s.dma_start(out=out[tt * 128:(tt + 1) * 128, :], in_=ot[:, tt % 2, :]).then_inc(sst.sem(tt), 16)
            sst.wait(s, NT - 2)
            sst.wait(s, NT - 1)

        @blk.scalar
        def _(a):
            for b in range(NB):
                for c in range(NCH):
                    k = b * NCH + c
                    if c == 0:
                        a.wait_ge(ld, 64 * (b + 1))
                    if k >= 2:
                        a.wait_ge(sqf, k - 1)
                    a.activation(out=sq[:, k % 2, :], in_=hb[:, c, :], func=AF.Square).then_inc(sqr, 1)
                a.wait_ge(sqf, NCH * (b + 1))
                a.activation(out=rstd[:], in_=ssp[:], func=AF.Sqrt, bias=P.epsb[:], scale=1.0 / D).then_inc(rs, 1)

        @blk.tensor
        def _(t):
            for b in range(NB):
                if b >= 1:
                    t.wait_ge(vd, b)
                for c in range(NCH):
                    k = b * NCH + c
                    t.wait_ge(sqr, k + 1)
                    t.matmul(ssp[:], lhsT=P.onesf[:], rhs=sq[:, k % 2, :], start=(c == 0), stop=(c == NCH - 1)).then_inc(sqf, 1)
                t.wait_ge(vd, b + 1)
                for t4 in range(4):
                    tt = b * 4 + t4
                    for q in range(8):
                        k = tt * 8 + q
                        if k >= 4:
                            t.wait_ge(scp, k - 3)
                        for i in range(4):
                            c = q * 4 + i
                            ins = t.transpose(tp[k % 4][:, i, :], hb[:, c, t4 * 128:(t4 + 1) * 128], P.idf[:])
                        ins.then_inc(stp, 1)

        @blk.vector
        def _(v):
            for b in range(NB):
                v.wait_ge(rs, b + 1)
                v.reciprocal(rstd[:], rstd[:])
                for c in range(NCH):
                    ins = v.scalar_tensor_tensor(out=hb[:, c, :], in0=hb[:, c, :],
                                                 scalar=P.cols[:, COL_FINAL + c:COL_FINAL + c + 1], in1=rstd[:],
                                                 op0=ALU.mult, op1=ALU.mult)
                ins.then_inc(vd, 1)
                for t4 in range(4):
                    tt = b * 4 + t4
                    for q in range(8):
                        k = tt * 8 + q
                        v.wait_ge(stp, k + 1)
                        if q == 0 and tt >= 2:
                            sst.wait(v, tt - 2)
                        v.tensor_copy(ot[:, tt % 2, q * 512:(q + 1) * 512].rearrange("p (i c) -> p i c", i=4), tp[k % 4][:]).then_inc(scp, 1)
    nc.all_engine_barrier()


def _tile_in(W, n):
    return np.ascontiguousarray(W.reshape(32, 128, n, 128).transpose(2, 1, 0, 3)).reshape(n * 128, 4096)


def _tile_ffn_in(W):
    a = W.reshape(32, 128, 2, 48, 128).transpose(3, 2, 1, 0, 4)
    return np.ascontiguousarray(a).reshape(96 * 128, 4096)


def _tile_ffn_out(W):
    a = W.reshape(2, 24, 128, 32, 128).transpose(0, 3, 2, 1, 4)
    return np.ascontiguousarray(a).reshape(64 * 128, 3072)


def host_consts():
    kl = np.arange(128)[:, None]
    d = (np.arange(TABW)[None, :] - TABOFF) - kl
    ad = np.abs(d)
    dist = ad.astype(np.float32)
    maskc = (ad <= 128).astype(np.float32)
    maskd = ((ad <= 64).astype(np.float32) + ((d % 4 == 0) & (ad <= 256)).astype(np.float32)
             + ((d % 16 == 0) & (ad <= 1024)).astype(np.float32))
    maska = np.zeros((128, 4, 8, 512), np.float32)
    rk, ck = np.arange(128) // 64, np.arange(128) % 64
    for g in range(4):
        j0 = NA_J0[g]
        qtok = g * 512 + np.arange(512)
        r, c = qtok // 64, qtok % 64
        rs = np.clip(r - 4, 0, 24)
        cs = np.clip(c - 8, 0, 48)
        for jj in range(NA_NJ[g]):
            j = j0 + jj
            kr = 2 * j + rk
            okr = (kr[:, None] >= rs[None, :]) & (kr[:, None] < rs[None, :] + 8)
            okc = (ck[:, None] >= cs[None, :]) & (ck[:, None] < cs[None, :] + 16)
            maska[:, g, jj, :] = (okr & okc)
    bf = ml_dtypes.bfloat16
    return dist, maskc.astype(bf), maskd.astype(bf), maska.reshape(128, -1).astype(bf)


NA_J0 = [0, 2, 6, 10]
NA_NJ = [6, 8, 8, 6]


def host_rtab(rpb):
    L = rpb.shape[0]
    rk, ck = np.arange(128) // 64, np.arange(128) % 64
    delta = 5 - np.arange(11)
    dr = 2 * delta[None, :, None] + rk[:, None, None] - rk[None, None, :]
    dc = ck[:, None, None] - ck[None, None, :] + 0 * delta[None, :, None]
    ir = np.clip(dr + 7, 0, 14)
    ic = np.clip(dc + 15, 0, 30)
    t = rpb[:, :, ir, ic]
    return np.ascontiguousarray(t).reshape(L * 8 * 128, 11 * 128).astype(np.float32)


def host_cols(inp, L):
    cols = np.zeros((128, NCOLS), np.float32)
    fm = lambda v: v.reshape(-1, 128).T
    for l in range(L):
        b = l * LCOLS
        cols[:, b:b + 32] = fm(inp["ffn1_norm"][l])
        cols[:, b + 32:b + 64] = fm(inp["mix_norm"][l])
        cols[:, b + 64:b + 96] = fm(inp["ffn2_norm"][l])
        cols[:, b + 96:b + 128] = fm(inp["branch_norm"][l].reshape(-1))
        cols[:, b + 128:b + 136] = fm(inp["conv_b"][l])
        cols[:, b + 136:b + 144] = fm(inp["conv_ln_g"][l])
        cols[:, b + 144:b + 152] = fm(inp["conv_ln_b"][l])
        cw = inp["conv_w"][l]
        cols[:, b + 152:b + 400] = cw.T.reshape(8, 128, 31).transpose(1, 0, 2).reshape(128, 248)
        cols[:, COL_SINK + l * 8:COL_SINK + l * 8 + 8] = inp["swa_sink"][l][None, :]
    cols[:, COL_FINAL:COL_FINAL + 32] = fm(inp["final_norm"])
    return cols


def host_prepare(inp, L):
    shared = {}
    shared["w1i"] = np.concatenate([_tile_ffn_in(inp["ffn1_w_in"][l]) for l in range(L)], 0)
    shared["w1o"] = np.concatenate([_tile_ffn_out(inp["ffn1_w_out"][l]) for l in range(L)], 0)
    shared["wmi"] = np.concatenate([_tile_in(inp["w_in"][l], 76) for l in range(L)], 0)
    shared["wmo"] = np.concatenate([_tile_in(inp["w_out"][l], 32) for l in range(L)], 0)
    shared["w2i"] = np.concatenate([_tile_ffn_in(inp["ffn2_w_in"][l]) for l in range(L)], 0)
    shared["w2o"] = np.concatenate([_tile_ffn_out(inp["ffn2_w_out"][l]) for l in range(L)], 0)
    shared["cols"] = host_cols(inp, L)
    shared["ident"] = np.eye(128, dtype=np.float32)
    dist, maskc, maskd, maska = host_consts()
    shared["dist"], shared["maskc"], shared["maskd"], shared["maska"] = dist, maskc, maskd, maska
    shared["rtab"] = host_rtab(inp["na_rpb"][:L])
    return shared


def kernel(**inputs):
    inp = {k: np.asarray(v) for k, v in inputs.items()}
    L = 4
    B = inp["x"].shape[0]
    n = N_CORES
    per = B // n
    nc = build_program(L, nseq=per)
    shared = host_prepare(inp, L)
    in_maps = []
    for c in range(n):
        m = dict(shared)
        m["x"] = np.ascontiguousarray(inp["x"][c * per:(c + 1) * per]).reshape(per * S, D)
        in_maps.append(m)
    res = run_bass_kernel_spmd(nc, in_maps, core_ids=list(range(n)))
    outs = [np.asarray(r["out"]).reshape(per, S, D) for r in res.results]
    return np.concatenate(outs, axis=0).astype(np.float32)
```

```python
import numpy as np
from contextlib import ExitStack
import ml_dtypes
import concourse.bass as bass
import concourse.mybir as mybir
from concourse.bass_utils import run_bass_kernel_spmd

F32 = mybir.dt.float32
BF16 = mybir.dt.bfloat16
AF = mybir.ActivationFunctionType
ALU = mybir.AluOpType

S = 2048
D = 4096
FF = 6144
NCH = 32
TG = 1024
NTG = 2
UROWS = 9728
EPS = 1e-6
SCALE = float(128 ** -0.5)
LCOLS = 400
COL_FINAL = 1600
COL_SINK = 1632
NCOLS = 1664
TABW = 2944
TABOFF = 1408
N_CORES = 8
SLOPES = [float(2.0 ** (-8.0 * i / 16)) for i in range(1, 17)]

WKINDS = ["w1i", "w1o", "wmi", "wmo", "w2i", "w2o"]
WROWS = {"w1i": 96 * 128, "w1o": 64 * 128, "wmi": 76 * 128, "wmo": 32 * 128, "w2i": 96 * 128, "w2o": 64 * 128}
WWID = {"w1i": 4096, "w1o": 3072, "wmi": 4096, "wmo": 4096, "w2i": 4096, "w2o": 3072}


_UID = [0]


def _u(name):
    _UID[0] += 1
    return f"{name}_{_UID[0]}"


def _sbt(nc, name, shape, dt):
    return nc.sbuf_tensor(_u(name), list(shape), dt)


def _pst(nc, name, shape, dt):
    return nc.psum_tensor(_u(name), list(shape), dt)


_LIVE = []


class _SemCtx:
    def __init__(self, h):
        self.h = h

    def __enter__(self):
        return self.h

    def __exit__(self, *a):
        return False


def _sem(nc, name):
    h = nc.alloc_semaphore(name=_u(name))
    _LIVE.append(h)
    return _SemCtx(h)


class SlotSem:
    def __init__(self, nc, ps, name, n, per_item=1):
        self.sems = [ps.enter_context(_sem(nc, f"{name}{i}")) for i in range(n)]
        self.n = n
        self.per = per_item

    def sem(self, i):
        return self.sems[i % self.n]

    def val(self, i):
        return 16 * self.per * (i // self.n + 1)

    def wait(self, eng, i):
        eng.wait_ge(self.sem(i), self.val(i))


def _end_block(nc):
    def f():
        nc.all_engine_barrier()
        if _LIVE:
            nc.clear_and_free_semaphores(list(_LIVE))
            _LIVE.clear()
        nc.all_engine_barrier()
    return f


class Prog:
    pass


def build_program(depth, phases=("ffn1", "mix", "ffn2"), mix_parts=("A", "B", "C", "D"), nseq=1):
    nc = bass.Bass("TRN2", target_bir_lowering=False)
    P = Prog()
    P.nc = nc
    P.depth = depth
    L = depth
    dt_in = lambda name, shape, dt=F32: nc.dram_tensor(name, list(shape), dt, kind="ExternalInput").ap()
    x_all = dt_in("x", [nseq * S, D])
    out_all = nc.dram_tensor("out", [nseq * S, D], F32, kind="ExternalOutput").ap()
    P.w32 = {k: dt_in(k, [L * WROWS[k], WWID[k]]) for k in WKINDS}
    P.wbf = {k: [nc.dram_tensor(f"{k}_bf{l}", [WROWS[k], WWID[k]], BF16, kind="Internal").ap() for l in range(L)] for k in WKINDS}
    cols_d = dt_in("cols", [128, NCOLS])
    idf_d = dt_in("ident", [128, 128])
    dist_d = dt_in("dist", [128, TABW])
    maskc_d = dt_in("maskc", [128, TABW], BF16)
    maskd_d = dt_in("maskd", [128, TABW], BF16)
    maska_d = dt_in("maska", [128, 4 * 8 * 512], BF16)
    rtab_d = dt_in("rtab", [L * 8 * 128, 11 * 128])
    hT = nc.dram_tensor("hT", [D, S], F32, kind="Internal").ap()
    uT = nc.dram_tensor("uT", [UROWS, S], BF16, kind="Internal").ap()
    mT = nc.dram_tensor("mT", [D, S], BF16, kind="Internal").ap()
    ssd = nc.dram_tensor("ssd", [4 * 128, S], F32, kind="Internal").ap()
    P.hT, P.uT, P.mT, P.ssd = hT, uT, mT, ssd
    P.dist_d, P.maskc_d, P.maskd_d, P.maska_d, P.rtab_d = dist_d, maskc_d, maskd_d, maska_d, rtab_d

    es = ExitStack()
    with es:
        sb = lambda name, shape, dt: es.enter_context(_sbt(nc, name, list(shape), dt))
        cols = sb("cols", [128, NCOLS], F32)
        idf = sb("idf", [128, 128], F32)
        idb = sb("idb", [128, 128], BF16)
        onesf = sb("onesf", [128, 128], F32)
        onesb = sb("onesb", [128, 128], BF16)
        esink = sb("esink", [128, 32], F32)
        epsb = sb("epsb", [128, 1], F32)
        P.epsb = epsb
        P.cols, P.idf, P.idb, P.onesf, P.onesb, P.esink = cols, idf, idb, onesf, onesb, esink
        _LIVE.clear()
        scrub = [nc.alloc_semaphore(name=f"scrub{i}") for i in range(24)]
        nc.all_engine_barrier()
        nc.clear_and_free_semaphores(scrub)
        nc.all_engine_barrier()
        castsem = [nc.alloc_semaphore(name=f"cast{i}") for i in range(3)]
        P.castsem = castsem
        stages = [(l, k) for l in range(L) for k in WKINDS]
        P.stage_idx = {st: i for i, st in enumerate(stages)}
        cast_cum = [0, 0, 0]
        P.stage_target = {}
        P.stage_ndma = {}
        CCH = 2048
        for i, (l, k) in enumerate(stages):
            nd = -(-WROWS[k] // CCH)
            cast_cum[i % 3] += 16 * nd
            P.stage_target[i] = cast_cum[i % 3]
        P.issued = 0

        def issue_casts(g, upto):
            upto = min(upto, len(stages) - 1)
            while P.issued <= upto:
                i = P.issued
                l, k = stages[i]
                r0 = l * WROWS[k]
                for r in range(0, WROWS[k], CCH):
                    rr = min(CCH, WROWS[k] - r)
                    g.dma_start(out=P.wbf[k][l][r:r + rr, :], in_=P.w32[k][r0 + r:r0 + r + rr, :]).then_inc(castsem[i % 3], 16)
                P.issued += 1
        P.issue_casts = issue_casts

        def cast_wait(eng, l, k):
            i = P.stage_idx[(l, k)]
            eng.wait_ge(castsem[i % 3], P.stage_target[i])
        P.cast_wait = cast_wait

        for sq in range(nseq):
            x = x_all[sq * S:(sq + 1) * S, :]
            out = out_all[sq * S:(sq + 1) * S, :]
            first = sq == 0
            with ExitStack() as ps:
                xt = ps.enter_context(_sbt(nc, "xt", [128, 2, D], F32))
                hst = ps.enter_context(_sbt(nc, "hst", [128, 2, NCH, 128], F32))
                tp = [ps.enter_context(_pst(nc, f"tp{i}", [128, 4, 128], F32)) for i in range(4)]
                sld = SlotSem(nc, ps, "p0ld", 2)
                sc = ps.enter_context(_sem(nc, "p0c"))
                stp = ps.enter_context(_sem(nc, "p0tp"))
                scp = ps.enter_context(_sem(nc, "p0cp"))
                sst = SlotSem(nc, ps, "p0st", 2)
                ps.callback(_end_block(nc))
                blk = ps.enter_context(nc.Block())
                NT = S // 128

                @blk.gpsimd
                def _(g):
                    if first:
                        issue_casts(g, len(stages) - 1)
                        for i in range(len(stages)):
                            g.wait_ge(castsem[i % 3], P.stage_target[i])

                @blk.sync
                def _(s):
                    if first:
                        s.dma_start(out=cols[:], in_=cols_d).then_inc(sc, 16)
                        s.dma_start(out=idf[:], in_=idf_d).then_inc(sc, 16)
                    for tt in range(NT + 1):
                        if tt < NT:
                            if tt >= 2:
                                s.wait_ge(stp, 8 * (tt - 1))
                            s.dma_start(out=xt[:, tt % 2, :], in_=x[tt * 128:(tt + 1) * 128, :]).then_inc(sld.sem(tt), 16)
                        if tt >= 1:
                            s.wait_ge(scp, 8 * tt)
                            s.dma_start(out=hT.rearrange("(c p) t -> p c t", p=128)[:, :, (tt - 1) * 128:tt * 128],
                                        in_=hst[:, (tt - 1) % 2, :, :]).then_inc(sst.sem(tt - 1), 16)
                    sst.wait(s, NT - 2)
                    sst.wait(s, NT - 1)

                @blk.vector
                def _(v):
                    if first:
                        v.wait_ge(sc, 32)
                        v.tensor_copy(idb[:], idf[:])
                        v.memset(onesf[:], 1.0)
                        v.memset(onesb[:], 1.0)
                        v.memset(epsb[:], EPS)
                    for tt in range(NT):
                        for q in range(8):
                            k = tt * 8 + q
                            v.wait_ge(stp, k + 1)
                            if q == 0 and tt >= 2:
                                sst.wait(v, tt - 2)
                            v.tensor_copy(hst[:, tt % 2, q * 4:(q + 1) * 4, :], tp[k % 4][:]).then_inc(scp, 1)

                @blk.scalar
                def _(a):
                    if first:
                        a.wait_ge(sc, 32)
                        a.activation(out=esink[:, 0:8 * L], in_=cols[:, COL_SINK:COL_SINK + 8 * L], func=AF.Exp)

                @blk.tensor
                def _(t):
                    if first:
                        t.wait_ge(sc, 32)
                    for tt in range(NT):
                        sld.wait(t, tt)
                        for q in range(8):
                            k = tt * 8 + q
                            if k >= 4:
                                t.wait_ge(scp, k - 3)
                            for i in range(4):
                                c = q * 4 + i
                                ins = t.transpose(tp[k % 4][:, i, :], xt[:, tt % 2, c * 128:(c + 1) * 128], idf[:])
                            ins.then_inc(stp, 1)
            nc.all_engine_barrier()

            for l in range(L):
                if "ffn1" in phases:
                    ffn(P, l, 0)
                if "mix" in phases:
                    mixer(P, l, mix_parts)
                if "ffn2" in phases:
                    ffn(P, l, 1)

            final_phase(P, out)
    return nc


def norm_block(P, tg, colbase, xnT):
    nc = P.nc
    with ExitStack() as ps:
        hb = ps.enter_context(_sbt(nc, "nb_h", [128, NCH, 512], F32))
        sq = ps.enter_context(_sbt(nc, "nb_sq", [128, 2, 512], F32))
        rstd = ps.enter_context(_sbt(nc, "nb_rstd", [128, 512], F32))
        ssp = ps.enter_context(_pst(nc, "nb_ss", [128, 512], F32))
        ld = ps.enter_context(_sem(nc, "nb_ld"))
        sqr = ps.enter_context(_sem(nc, "nb_sqr"))
        sqf = ps.enter_context(_sem(nc, "nb_sqf"))
        vd = ps.enter_context(_sem(nc, "nb_vd"))
        rs = ps.enter_context(_sem(nc, "nb_rs"))
        ps.callback(_end_block(nc))
        blk = ps.enter_context(nc.Block())
        hv = P.hT.rearrange("(c p) t -> p c t", p=128)

        @blk.sync
        def _(s):
            for th in range(2):
                if th >= 1:
                    s.wait_ge(vd, th)
                t0 = tg * TG + th * 512
                for q in range(4):
                    s.dma_start(out=hb[:, q * 8:(q + 1) * 8, :], in_=hv[:, q * 8:(q + 1) * 8, t0:t0 + 512]).then_inc(ld, 16)

        @blk.scalar
        def _(a):
            for th in range(2):
                for c in range(NCH):
                    k = th * NCH + c
                    if c == 0:
                        a.wait_ge(ld, 64 * (th + 1))
                    if k >= 2:
                        a.wait_ge(sqf, k - 1)
                    a.activation(out=sq[:, k % 2, :], in_=hb[:, c, :], func=AF.Square).then_inc(sqr, 1)
                a.wait_ge(sqf, NCH * (th + 1))
                a.activation(out=rstd[:], in_=ssp[:], func=AF.Sqrt, bias=P.epsb[:], scale=1.0 / D).then_inc(rs, 1)

        @blk.tensor
        def _(t):
            for th in range(2):
                if th >= 1:
                    t.wait_ge(vd, th)
                for c in range(NCH):
                    k = th * NCH + c
                    t.wait_ge(sqr, k + 1)
                    t.matmul(ssp[:], lhsT=P.onesf[:], rhs=sq[:, k % 2, :], start=(c == 0), stop=(c == NCH - 1)).then_inc(sqf, 1)

        @blk.vector
        def _(v):
            for th in range(2):
                v.wait_ge(rs, th + 1)
                v.reciprocal(rstd[:], rstd[:])
                for c in range(NCH):
                    ins = v.scalar_tensor_tensor(out=xnT[:, c, th * 512:(th + 1) * 512], in0=hb[:, c, :],
                                                 scalar=P.cols[:, colbase + c:colbase + c + 1], in1=rstd[:],
                                                 op0=ALU.mult, op1=ALU.mult)
                ins.then_inc(vd, 1)
    nc.all_engine_barrier()


def emit_wload(P, s, wsrc, piece0, NP, KC, wring, wld, pready, l, kind):
    NW = len(wring)
    P.cast_wait(s, l, kind)
    wsrc = P.wbf[kind][l]
    piece0 = piece0 - l * (WROWS[kind] // 128)
    for p in range(NP):
        if p >= NW:
            s.wait_ge(pready, p - NW + 1)
        r = (piece0 + p) * 128
        s.dma_start(out=wring[p % NW][:, 0:KC * 128], in_=wsrc[r:r + 128, :]).then_inc(wld.sem(p), 16)


def emit_mm(t, NP, KC, inT, wring, wld, psl, pready, pfree_fn):
    NW = len(wring)
    NPB = len(psl)
    for p in range(NP):
        wld.wait(t, p)
        pfree_fn(t, p)
        w = wring[p % NW]
        for kc in range(KC):
            for th in range(2):
                ins = t.matmul(psl[p % NPB][th][:], lhsT=w[:, kc * 128:(kc + 1) * 128],
                               rhs=inT[:, kc, th * 512:(th + 1) * 512], start=(kc == 0), stop=(kc == KC - 1))
        ins.then_inc(pready, 1)


def gemm_resid_block(P, tg, inT, KC, l, kind, piece0, alpha):
    nc = P.nc
    NP = NCH
    NR = 3
    with ExitStack() as ps:
        wring = [ps.enter_context(_sbt(nc, f"gr_w{i}", [128, 4096], BF16)) for i in range(3)]
        rr = ps.enter_context(_sbt(nc, "gr_r", [128, NR, TG], F32))
        psl = [[ps.enter_context(_pst(nc, f"gr_ps{i}_{th}", [128, 512], F32)) for th in range(2)] for i in range(3)]
        wld = SlotSem(nc, ps, "gr_wld", 3)
        pready = ps.enter_context(_sem(nc, "gr_pr"))
        rld = SlotSem(nc, ps, "gr_rld", 3)
        vd = ps.enter_context(_sem(nc, "gr_vd"))
        st = SlotSem(nc, ps, "gr_st", 3)
        ps.callback(_end_block(nc))
        blk = ps.enter_context(nc.Block())
        t0 = tg * TG

        @blk.sync
        def _(s):
            emit_wload(P, s, P.wbf[kind], piece0, NP, KC, wring, wld, pready, l, kind)

        @blk.tensor
        def _(t):
            def pfree(t, p):
                if p >= 3:
                    t.wait_ge(vd, p - 2)
            emit_mm(t, NP, KC, inT, wring, wld, psl, pready, pfree)

        @blk.scalar
        def _(a):
            for c in range(NP + 1):
                if c < NP:
                    if c >= NR:
                        st.wait(a, c - NR)
                    a.dma_start(out=rr[:, c % NR, :], in_=P.hT[c * 128:(c + 1) * 128, t0:t0 + TG]).then_inc(rld.sem(c), 16)
                if c >= 1:
                    a.wait_ge(vd, c)
                    a.dma_start(out=P.hT[(c - 1) * 128:c * 128, t0:t0 + TG], in_=rr[:, (c - 1) % NR, :]).then_inc(st.sem(c - 1), 16)
            for c in range(NP - NR, NP):
                st.wait(a, c)

        @blk.vector
        def _(v):
            for c in range(NP):
                v.wait_ge(pready, c + 1)
                rld.wait(v, c)
                for th in range(2):
                    ins = v.scalar_tensor_tensor(out=rr[:, c % NR, th * 512:(th + 1) * 512], in0=psl[c % 3][th][:],
                                                 scalar=alpha, in1=rr[:, c % NR, th * 512:(th + 1) * 512],
                                                 op0=ALU.mult, op1=ALU.add)
                ins.then_inc(vd, 1)
    nc.all_engine_barrier()


def ffn(P, l, which):
    nc = P.nc
    ki = "w1i" if which == 0 else "w2i"
    ko = "w1o" if which == 0 else "w2o"
    colbase = l * LCOLS + (0 if which == 0 else 64)
    with ExitStack() as outer:
        xnT = outer.enter_context(_sbt(nc, "ffn_xn", [128, NCH, TG], BF16))
        for tg in range(NTG):
            norm_block(P, tg, colbase, xnT)
            with ExitStack() as mid:
                h1T = mid.enter_context(_sbt(nc, "ffn_h1", [128, 24, TG], BF16))
                for hf in range(2):
                    with ExitStack() as ps:
                        wring = [ps.enter_context(_sbt(nc, f"g1_w{i}", [128, 4096], BF16)) for i in range(3)]
                        tmp = ps.enter_context(_sbt(nc, "g1_tmp", [128, 2, TG], F32))
                        psl = [[ps.enter_context(_pst(nc, f"g1_ps{i}_{th}", [128, 512], F32)) for th in range(2)] for i in range(4)]
                        wld = SlotSem(nc, ps, "g1_wld", 3)
                        pready = ps.enter_context(_sem(nc, "g1_pr"))
                        fa = ps.enter_context(_sem(nc, "g1_fa"))
                        fv = ps.enter_context(_sem(nc, "g1_fv"))
                        ps.callback(_end_block(nc))
                        blk = ps.enter_context(nc.Block())
                        NP = 48
                        piece0 = l * 96 + hf * 48

                        @blk.sync
                        def _(s):
                            emit_wload(P, s, P.wbf[ki], piece0, NP, NCH, wring, wld, pready, l, ki)

                        @blk.tensor
                        def _(t):
                            def pfree(t, p):
                                if p >= 4:
                                    q = p - 4
                                    t.wait_ge(fa if q % 2 == 0 else fv, q // 2 + 1)
                            emit_mm(t, NP, NCH, xnT, wring, wld, psl, pready, pfree)

                        @blk.scalar
                        def _(a):
                            for jl in range(24):
                                a.wait_ge(pready, 2 * jl + 1)
                                if jl >= 2:
                                    a.wait_ge(fv, jl - 1)
                                for th in range(2):
                                    ins = a.activation(out=tmp[:, jl % 2, th * 512:(th + 1) * 512],
                                                       in_=psl[(2 * jl) % 4][th][:], func=AF.Silu)
                                ins.then_inc(fa, 1)

                        @blk.vector
                        def _(v):
                            for jl in range(24):
                                v.wait_ge(pready, 2 * jl + 2)
                                v.wait_ge(fa, jl + 1)
                                for th in range(2):
                                    ins = v.tensor_tensor(out=h1T[:, jl, th * 512:(th + 1) * 512],
                                                          in0=tmp[:, jl % 2, th * 512:(th + 1) * 512],
                                                          in1=psl[(2 * jl + 1) % 4][th][:], op=ALU.mult)
                                ins.then_inc(fv, 1)
                    nc.all_engine_barrier()
                    gemm_resid_block(P, tg, h1T, 24, l, ko, l * 64 + hf * 32, 0.5)


def inproj_block(P, tg, xnT, l):
    nc = P.nc
    NP = 76
    with ExitStack() as ps:
        wring = [ps.enter_context(_sbt(nc, f"ip_w{i}", [128, 4096], BF16)) for i in range(3)]
        stg = ps.enter_context(_sbt(nc, "ip_stg", [128, 3, TG], BF16))
        psl = [[ps.enter_context(_pst(nc, f"ip_ps{i}_{th}", [128, 512], F32)) for th in range(2)] for i in range(3)]
        wld = SlotSem(nc, ps, "ip_wld", 3)
        pready = ps.enter_context(_sem(nc, "ip_pr"))
        ca = ps.enter_context(_sem(nc, "ip_ca"))
        cv = ps.enter_context(_sem(nc, "ip_cv"))
        ust = SlotSem(nc, ps, "ip_ust", 3)
        ps.callback(_end_block(nc))
        blk = ps.enter_context(nc.Block())
        t0 = tg * TG

        @blk.sync
        def _(s):
            wsrc = P.wbf["wmi"][l]
            for p in range(NP + 2):
                if p < NP:
                    if p >= 3:
                        s.wait_ge(pready, p - 2)
                    s.dma_start(out=wring[p % 3][:, 0:NCH * 128], in_=wsrc[p * 128:(p + 1) * 128, :]).then_inc(wld.sem(p), 16)
                n = p - 2
                if n >= 0:
                    s.wait_ge(ca, n + 1)
                    s.wait_ge(cv, n + 1)
                    s.dma_start(out=P.uT[n * 128:(n + 1) * 128, t0:t0 + TG], in_=stg[:, n % 3, :]).then_inc(ust.sem(n), 16)
            for n in range(NP - 3, NP):
                ust.wait(s, n)

        @blk.tensor
        def _(t):
            def pfree(t, p):
                if p >= 3:
                    t.wait_ge(ca, p - 2)
                    t.wait_ge(cv, p - 2)
            emit_mm(t, NP, NCH, xnT, wring, wld, psl, pready, pfree)

        @blk.scalar
        def _(a):
            for n in range(NP):
                a.wait_ge(pready, n + 1)
                if n >= 3:
                    ust.wait(a, n - 3)
                a.copy(stg[:, n % 3, 0:512], psl[n % 3][0][:]).then_inc(ca, 1)

        @blk.vector
        def _(v):
            for n in range(NP):
                v.wait_ge(pready, n + 1)
                if n >= 3:
                    ust.wait(v, n - 3)
                v.tensor_copy(stg[:, n % 3, 512:1024], psl[n % 3][1][:]).then_inc(cv, 1)
    nc.all_engine_barrier()


ATT = {
    "A": dict(q=0, k=1024, v=2048, nkv=8, m=0, ss=0),
    "C": dict(q=5120, k=6144, v=6400, nkv=2, m=2048, ss=2),
    "D": dict(q=6656, k=7680, v=8704, nkv=8, m=3072, ss=3),
}


def att_keytiles(kind, g):
    if kind == "A":
        return list(range(NA_J0[g], NA_J0[g] + NA_NJ[g]))
    if kind == "C":
        return list(range(max(0, 4 * g - 1), min(16, 4 * g + 5)))
    return list(range(max(0, 4 * g - 8), min(16, 4 * g + 12)))


def attn_block(P, l, kind):
    nc = P.nc
    cfg = ATT[kind]
    isA = kind == "A"
    units = []
    for g in range(4):
        js = att_keytiles(kind, g)
        for j in js:
            units.append((g, j, j == js[0], j == js[-1]))
    U = len(units)
    LDH = 64 if isA else 48
    TBLN = 128 if isA else 32
    with ExitStack() as ps:
        qb = ps.enter_context(_sbt(nc, "at_q", [128, 2, S], BF16))
        kb = ps.enter_context(_sbt(nc, "at_k", [128, 2, S], BF16))
        vb = ps.enter_context(_sbt(nc, "at_v", [128, 2, S], BF16))
        V = ps.enter_context(_sbt(nc, "at_V", [128, 2, 16, 128], BF16))
        if isA:
            rt = ps.enter_context(_sbt(nc, "at_rt", [128, 2, 11 * 128], F32))
            mk = ps.enter_context(_sbt(nc, "at_ma", [128, 4 * 8 * 512], BF16))
        else:
            dist = ps.enter_context(_sbt(nc, "at_dist", [128, TABW], F32))
            mk = ps.enter_context(_sbt(nc, "at_mk", [128, TABW], BF16))
        tmp = ps.enter_context(_sbt(nc, "at_tmp", [128, 3, 512], F32))
        pr = ps.enter_context(_sbt(nc, "at_pr", [128, 3, 512], F32))
        pp = ps.enter_context(_sbt(nc, "at_pp", [128, 3, 512], BF16))
        den = ps.enter_context(_sbt(nc, "at_den", [128, 512], F32))
        osb = ps.enter_context(_sbt(nc, "at_o", [128, 2, 512], F32))
        osq = ps.enter_context(_sbt(nc, "at_osq", [128, 512], BF16))
        stage = ps.enter_context(_sbt(nc, "at_stage", [128, 2, S], BF16))
        ssacc = ps.enter_context(_sbt(nc, "at_ssacc", [128, S], F32))
        s_ps = [ps.enter_context(_pst(nc, f"at_s{i}", [128, 512], F32)) for i in range(2)]
        o_ps = [ps.enter_context(_pst(nc, f"at_ops{i}", [128, 512], F32)) for i in range(2)]
        d_ps = [ps.enter_context(_pst(nc, f"at_dps{i}", [128, 512], F32)) for i in range(2)]
        ss_ps = ps.enter_context(_pst(nc, "at_ssps", [128, 512], F32))
        tp_ps = ps.enter_context(_pst(nc, "at_tp", [128, 8, 128], BF16))
        sem = lambda n: ps.enter_context(_sem(nc, "at_" + n))
        tbl, vtp, vcp, sr, tf, er, ppr, pvd, od, sqd, ssr, ssf, sso = [sem(n) for n in
            ("tbl", "vtp", "vcp", "sr", "tf", "er", "ppr", "pvd", "od", "sqd", "ssr", "ssf", "sso")]
        ld = SlotSem(nc, ps, "at_ld", 2, per_item=LDH // 16)
        mst = SlotSem(nc, ps, "at_mst", 2)
        ps.callback(_end_block(nc))
        blk = ps.enter_context(nc.Block())
        NH = 8

        def mask_ap(g, j):
            if isA:
                jj = j - NA_J0[g]
                o = (g * 8 + jj) * 512
            else:
                o = 512 * g - 128 * j + TABOFF
            return mk[:, o:o + 512]

        @blk.sync
        def _(s):
            if isA:
                for q4 in range(8):
                    s.dma_start(out=mk[:, q4 * 2048:(q4 + 1) * 2048], in_=P.maska_d[:, q4 * 2048:(q4 + 1) * 2048]).then_inc(tbl, 16)
            else:
                s.dma_start(out=dist[:], in_=P.dist_d).then_inc(tbl, 16)
                s.dma_start(out=mk[:], in_=(P.maskc_d if kind == "C" else P.maskd_d)).then_inc(tbl, 16)
            s.wait_ge(tbl, TBLN)
            for hh in range(NH):
                if hh >= 2:
                    s.wait_ge(pvd, (hh - 1) * U)
                kvh = hh if cfg["nkv"] == 8 else hh // 4
                s.dma_start(out=qb[:, hh % 2, :], in_=P.uT[cfg["q"] + hh * 128:cfg["q"] + (hh + 1) * 128, :]).then_inc(ld.sem(hh), 16)
                s.dma_start(out=kb[:, hh % 2, :], in_=P.uT[cfg["k"] + kvh * 128:cfg["k"] + (kvh + 1) * 128, :]).then_inc(ld.sem(hh), 16)
                s.dma_start(out=vb[:, hh % 2, :], in_=P.uT[cfg["v"] + kvh * 128:cfg["v"] + (kvh + 1) * 128, :]).then_inc(ld.sem(hh), 16)
                if isA:
                    r0 = (l * 8 + hh) * 128
                    s.dma_start(out=rt[:, hh % 2, :], in_=P.rtab_d[r0:r0 + 128, :]).then_inc(ld.sem(hh), 16)
                if hh >= 1:
                    s.wait_ge(sqd, 4 * hh)
                    s.dma_start(out=P.mT[cfg["m"] + (hh - 1) * 128:cfg["m"] + hh * 128, :], in_=stage[:, (hh - 1) % 2, :]).then_inc(mst.sem(hh - 1), 16)
            s.wait_ge(sqd, 4 * NH)
            s.dma_start(out=P.mT[cfg["m"] + (NH - 1) * 128:cfg["m"] + NH * 128, :], in_=stage[:, (NH - 1) % 2, :]).then_inc(mst.sem(NH - 1), 16)
            s.wait_ge(ssf, 4 * NH)
            s.dma_start(out=P.ssd[cfg["ss"] * 128:(cfg["ss"] + 1) * 128, :], in_=ssacc[:, :]).then_inc(sso, 16)
            mst.wait(s, NH - 2)
            mst.wait(s, NH - 1)
            s.wait_ge(sso, 16)

        @blk.gpsimd
        def _(gp):
            gp.wait_ge(tbl, TBLN)
            for hh in range(NH):
                for u, (g, j, first, last) in enumerate(units):
                    gu = hh * U + u
                    gp.wait_ge(er, gu + 1)
                    if gu >= 3:
                        gp.wait_ge(pvd, gu - 2)
                    gp.tensor_tensor(out=pp[:, gu % 3, :], in0=pr[:, gu % 3, :], in1=mask_ap(g, j), op=ALU.mult).then_inc(ppr, 1)

        @blk.tensor
        def _(t):
            def ssmm(G):
                t.wait_ge(sqd, G + 1)
                if G >= 1:
                    t.wait_ge(ssf, G)
                t.matmul(ss_ps[:], lhsT=P.onesb[:], rhs=osq[:], start=True, stop=True).then_inc(ssr, 1)
            for hh in range(NH):
                ld.wait(t, hh)
                for b in range(2):
                    k = 2 * hh + b
                    if k >= 1:
                        t.wait_ge(vcp, k)
                    for i in range(8):
                        ins = t.transpose(tp_ps[:, i, :], vb[:, hh % 2, (8 * b + i) * 128:(8 * b + i + 1) * 128], P.idb[:])
                    ins.then_inc(vtp, 1)
                t.wait_ge(vcp, 2 * (hh + 1))
                for uu in range(U + 1):
                    if uu < U:
                        g, j, first, last = units[uu]
                        gu = hh * U + uu
                        if gu >= 2:
                            t.wait_ge(tf, gu - 1)
                        t.matmul(s_ps[gu % 2][:], lhsT=kb[:, hh % 2, j * 128:(j + 1) * 128],
                                 rhs=qb[:, hh % 2, g * 512:(g + 1) * 512], start=True, stop=True).then_inc(sr, 1)
                    if uu >= 1:
                        g, j, first, last = units[uu - 1]
                        gu = hh * U + uu - 1
                        G = hh * 4 + g
                        t.wait_ge(ppr, gu + 1)
                        if first and G >= 2:
                            t.wait_ge(od, G - 1)
                        t.matmul(o_ps[G % 2][:], lhsT=V[:, hh % 2, j, :], rhs=pp[:, gu % 3, :], start=first, stop=last)
                        t.matmul(d_ps[G % 2][:], lhsT=P.onesb[:], rhs=pp[:, gu % 3, :], start=first, stop=last).then_inc(pvd, 1)
                        if last and G >= 1:
                            ssmm(G - 1)
            ssmm(4 * NH - 1)

        @blk.vector
        def _(v):
            v.wait_ge(tbl, TBLN)

            def post(hh, g, gu_last):
                G = hh * 4 + g
                v.wait_ge(pvd, gu_last + 1)
                if G >= 2:
                    v.wait_ge(sqd, G - 1)
                if kind == "C":
                    v.tensor_scalar(out=den[:], in0=d_ps[G % 2][:], scalar1=P.esink[:, l * 8 + hh:l * 8 + hh + 1], scalar2=None, op0=ALU.add)
                else:
                    v.tensor_copy(den[:], d_ps[G % 2][:])
                v.reciprocal(den[:], den[:])
                v.tensor_tensor(out=osb[:, G % 2, :], in0=o_ps[G % 2][:], in1=den[:], op=ALU.mult).then_inc(od, 1)

            def addss(G):
                hh, g = G // 4, G % 4
                v.wait_ge(ssr, G + 1)
                if hh == 0:
                    v.tensor_copy(ssacc[:, g * 512:(g + 1) * 512], ss_ps[:]).then_inc(ssf, 1)
                else:
                    v.tensor_tensor(out=ssacc[:, g * 512:(g + 1) * 512], in0=ssacc[:, g * 512:(g + 1) * 512], in1=ss_ps[:], op=ALU.add).then_inc(ssf, 1)

            for hh in range(NH):
                for b in range(2):
                    k = 2 * hh + b
                    v.wait_ge(vtp, k + 1)
                    v.tensor_copy(V[:, hh % 2, 8 * b:8 * b + 8, :], tp_ps[:]).then_inc(vcp, 1)
                if isA:
                    ld.wait(v, hh)
                pend = None
                for u, (g, j, first, last) in enumerate(units):
                    gu = hh * U + u
                    v.wait_ge(sr, gu + 1)
                    if gu >= 3:
                        v.wait_ge(er, gu - 2)
                    if isA:
                        i0 = 5 - (j - 4 * g)
                        v.scalar_tensor_tensor(out=tmp[:, gu % 3, :], in0=s_ps[gu % 2][:], scalar=SCALE,
                                               in1=rt[:, hh % 2, i0 * 128:i0 * 128 + 512], op0=ALU.mult, op1=ALU.add).then_inc(tf, 1)
                    else:
                        o = 512 * g - 128 * j + TABOFF
                        slope = SLOPES[hh] if kind == "C" else SLOPES[8 + hh]
                        v.scalar_tensor_tensor(out=tmp[:, gu % 3, :], in0=dist[:, o:o + 512], scalar=-slope / SCALE,
                                               in1=s_ps[gu % 2][:], op0=ALU.mult, op1=ALU.add).then_inc(tf, 1)
                    if pend is not None:
                        post(*pend)
                        Gp = pend[0] * 4 + pend[1]
                        if Gp >= 1:
                            addss(Gp - 1)
                        pend = None
                    if last:
                        pend = (hh, g, gu)
                post(*pend)
                Gp = pend[0] * 4 + pend[1]
                if Gp >= 1:
                    addss(Gp - 1)
            addss(4 * NH - 1)

        @blk.scalar
        def _(a):
            def post(hh, g):
                G = hh * 4 + g
                a.wait_ge(od, G + 1)
                if hh >= 2 and g == 0:
                    mst.wait(a, hh - 2)
                a.copy(stage[:, hh % 2, g * 512:(g + 1) * 512], osb[:, G % 2, :])
                if G >= 1:
                    a.wait_ge(ssr, G)
                a.activation(out=osq[:], in_=osb[:, G % 2, :], func=AF.Square).then_inc(sqd, 1)
            for hh in range(NH):
                pend = None
                cnt = 0
                for u, (g, j, first, last) in enumerate(units):
                    gu = hh * U + u
                    a.wait_ge(tf, gu + 1)
                    if gu >= 3:
                        a.wait_ge(ppr, gu - 2)
                    a.activation(out=pr[:, gu % 3, :], in_=tmp[:, gu % 3, :], func=AF.Exp,
                                 scale=(1.0 if isA else SCALE)).then_inc(er, 1)
                    if pend is not None:
                        cnt += 1
                        if cnt >= 2:
                            post(*pend)
                            pend = None
                    if last:
                        if pend is not None:
                            post(*pend)
                        pend = (hh, g)
                        cnt = 0
                post(*pend)
    nc.all_engine_barrier()


def conv_block(P, l):
    nc = P.nc
    cb = l * LCOLS
    with ExitStack() as ps:
        ab = ps.enter_context(_sbt(nc, "cv_a", [128, 2, S], BF16))
        gb = ps.enter_context(_sbt(nc, "cv_g", [128, 2, S], BF16))
        sg = ps.enter_context(_sbt(nc, "cv_sg", [128, 2, S], F32))
        hc = ps.enter_context(_sbt(nc, "cv_hc", [128, 2, S + 32], F32))
        cv = ps.enter_context(_sbt(nc, "cv_cv", [128, 8, S], F32))
        stage = ps.enter_context(_sbt(nc, "cv_stage", [128, 8, S], BF16))
        sqt = ps.enter_context(_sbt(nc, "cv_sqt", [128, 2, 512], F32))
        mean = ps.enter_context(_sbt(nc, "cv_mean", [128, 512], F32))
        msq = ps.enter_context(_sbt(nc, "cv_msq", [128, 512], F32))
        rstd = ps.enter_context(_sbt(nc, "cv_rstd", [128, 512], F32))
        t1 = ps.enter_context(_sbt(nc, "cv_t1", [128, 2, 512], F32))
        ob = ps.enter_context(_sbt(nc, "cv_ob", [128, 2, 512], F32))
        osq = ps.enter_context(_sbt(nc, "cv_osq", [128, 2, 512], F32))
        ssacc = ps.enter_context(_sbt(nc, "cv_ssacc", [128, S], F32))
        s1_ps = ps.enter_context(_pst(nc, "cv_s1", [128, 512], F32))
        s2_ps = ps.enter_context(_pst(nc, "cv_s2", [128, 512], F32))
        ss_ps = ps.enter_context(_pst(nc, "cv_ss", [128, 512], F32))
        sem = lambda n: ps.enter_context(_sem(nc, "cv_" + n))
        sig, cvdV, cvdP, sqr, sqf, vmr, rsr, t1r, obf, ssm, ssc, mst = [sem(n) for n in
            ("sig", "cvdV", "cvdP", "sqr", "sqf", "vmr", "rsr", "t1r", "obf", "ssm", "ssc", "mst")]
        ld = SlotSem(nc, ps, "cv_ld", 2, per_item=2)
        ps.callback(_end_block(nc))
        blk = ps.enter_context(nc.Block())
        AROW, GROW = 3072, 4096

        @blk.sync
        def _(s):
            for c in range(8):
                if c >= 2:
                    s.wait_ge(cvdV if c % 2 == 0 else cvdP, c // 2)
                s.dma_start(out=ab[:, c % 2, :], in_=P.uT[AROW + c * 128:AROW + (c + 1) * 128, :]).then_inc(ld.sem(c), 16)
                s.dma_start(out=gb[:, c % 2, :], in_=P.uT[GROW + c * 128:GROW + (c + 1) * 128, :]).then_inc(ld.sem(c), 16)
            s.wait_ge(obf, 32)
            for c in range(8):
                s.dma_start(out=P.mT[1024 + c * 128:1024 + (c + 1) * 128, :], in_=stage[:, c, :]).then_inc(mst, 16)
            s.wait_ge(ssc, 4)
            s.dma_start(out=P.ssd[128:256, :], in_=ssacc[:, :]).then_inc(mst, 16)
            s.wait_ge(mst, 16 * 9)

        def conv_chunk(e, c, done):
            e.wait_ge(sig, c + 1)
            e.tensor_tensor(out=hc[:, c % 2, 15:15 + S], in0=ab[:, c % 2, :], in1=sg[:, c % 2, :], op=ALU.mult)
            wc = cb + 152 + c * 31
            e.tensor_scalar(out=cv[:, c, :], in0=hc[:, c % 2, 0:S], scalar1=P.cols[:, wc:wc + 1],
                            scalar2=P.cols[:, cb + 128 + c:cb + 129 + c], op0=ALU.mult, op1=ALU.add)
            for tau in range(1, 31):
                ins = e.scalar_tensor_tensor(out=cv[:, c, :], in0=hc[:, c % 2, tau:tau + S], scalar=P.cols[:, wc + tau:wc + tau + 1],
                                             in1=cv[:, c, :], op0=ALU.mult, op1=ALU.add)
            ins.then_inc(done, 1)


        @blk.scalar
        def _(a):
            for c in range(8):
                ld.wait(a, c)
                if c >= 2:
                    a.wait_ge(cvdV if c % 2 == 0 else cvdP, c // 2)
                a.activation(out=sg[:, c % 2, :], in_=gb[:, c % 2, :], func=AF.Sigmoid).then_inc(sig, 1)
            a.wait_ge(cvdV, 4)
            a.wait_ge(cvdP, 4)
            for b in range(4):
                bs = slice(b * 512, (b + 1) * 512)
                for c in range(8):
                    k = b * 8 + c
                    if k >= 2:
                        a.wait_ge(sqf, k - 1)
                    a.activation(out=sqt[:, k % 2, :], in_=cv[:, c, bs], func=AF.Square).then_inc(sqr, 1)
                a.wait_ge(vmr, b + 1)
                a.activation(out=rstd[:], in_=rstd[:], func=AF.Sqrt, bias=P.epsb[:], scale=1.0).then_inc(rsr, 1)
                for c in range(8):
                    k = b * 8 + c
                    a.wait_ge(t1r, k + 1)
                    if k >= 2:
                        a.wait_ge(ssm, k - 1)
                    a.activation(out=ob[:, k % 2, :], in_=t1[:, k % 2, :], func=AF.Silu,
                                 bias=P.cols[:, cb + 144 + c:cb + 145 + c], scale=P.cols[:, cb + 136 + c:cb + 137 + c])
                    a.copy(stage[:, c, bs], ob[:, k % 2, :])
                    a.activation(out=osq[:, k % 2, :], in_=ob[:, k % 2, :], func=AF.Square).then_inc(obf, 1)

        @blk.tensor
        def _(t):
            for b in range(4):
                bs = slice(b * 512, (b + 1) * 512)
                if b >= 1:
                    t.wait_ge(vmr, b)
                for c in range(8):
                    k = b * 8 + c
                    t.wait_ge(sqr, k + 1)
                    t.matmul(s1_ps[:], lhsT=P.onesf[:], rhs=cv[:, c, bs], start=(c == 0), stop=(c == 7))
                    t.matmul(s2_ps[:], lhsT=P.onesf[:], rhs=sqt[:, k % 2, :], start=(c == 0), stop=(c == 7)).then_inc(sqf, 1)
                if b >= 1:
                    t.wait_ge(ssc, b)
                for c in range(8):
                    k = b * 8 + c
                    t.wait_ge(obf, k + 1)
                    t.matmul(ss_ps[:], lhsT=P.onesf[:], rhs=osq[:, k % 2, :], start=(c == 0), stop=(c == 7)).then_inc(ssm, 1)

        @blk.vector
        def _(v):
            v.memset(hc[:, 0, :], 0.0)
            v.memset(hc[:, 1, :], 0.0)
            for c in range(8):
                conv_chunk(v, c, cvdV if c % 2 == 0 else cvdP)
            v.wait_ge(cvdP, 4)
            for b in range(4):
                bs = slice(b * 512, (b + 1) * 512)
                v.wait_ge(sqf, 8 * (b + 1))
                v.tensor_scalar(out=mean[:], in0=s1_ps[:], scalar1=1.0 / 1024, scalar2=None, op0=ALU.mult)
                v.tensor_tensor(out=msq[:], in0=mean[:], in1=mean[:], op=ALU.mult)
                v.scalar_tensor_tensor(out=rstd[:], in0=s2_ps[:], scalar=1.0 / 1024, in1=msq[:], op0=ALU.mult, op1=ALU.subtract).then_inc(vmr, 1)
                v.wait_ge(rsr, b + 1)
                v.reciprocal(rstd[:], rstd[:])
                for c in range(8):
                    k = b * 8 + c
                    if k >= 2:
                        v.wait_ge(obf, k - 1)
                    v.tensor_tensor(out=t1[:, k % 2, :], in0=cv[:, c, bs], in1=mean[:], op=ALU.subtract)
                    v.tensor_tensor(out=t1[:, k % 2, :], in0=t1[:, k % 2, :], in1=rstd[:], op=ALU.mult).then_inc(t1r, 1)
                v.wait_ge(ssm, 8 * (b + 1))
                v.tensor_copy(ssacc[:, bs], ss_ps[:]).then_inc(ssc, 1)
    nc.all_engine_barrier()


def outproj_prologue(P, tg, l, xin):
    nc = P.nc
    cb = l * LCOLS + 96
    with ExitStack() as ps:
        ssb = ps.enter_context(_sbt(nc, "op_ss", [128, 4, TG], F32))
        mr = ps.enter_context(_sbt(nc, "op_m", [128, 4, TG], BF16))
        ld = ps.enter_context(_sem(nc, "op_ld"))
        ml = SlotSem(nc, ps, "op_ml", 4)
        sq = ps.enter_context(_sem(nc, "op_sq"))
        vd = ps.enter_context(_sem(nc, "op_vd"))
        ps.callback(_end_block(nc))
        blk = ps.enter_context(nc.Block())
        t0 = tg * TG

        @blk.sync
        def _(s):
            for gi in range(4):
                s.dma_start(out=ssb[:, gi, :], in_=P.ssd[gi * 128:(gi + 1) * 128, t0:t0 + TG]).then_inc(ld, 16)
            for kc in range(NCH):
                if kc >= 4:
                    s.wait_ge(vd, kc - 3)
                s.dma_start(out=mr[:, kc % 4, :], in_=P.mT[kc * 128:(kc + 1) * 128, t0:t0 + TG]).then_inc(ml.sem(kc), 16)

        @blk.scalar
        def _(a):
            a.wait_ge(ld, 64)
            for gi in range(4):
                ins = a.activation(out=ssb[:, gi, :], in_=ssb[:, gi, :], func=AF.Sqrt, bias=P.epsb[:], scale=1.0 / 1024)
            ins.then_inc(sq, 1)

        @blk.vector
        def _(v):
            v.wait_ge(sq, 1)
            for gi in range(4):
                v.reciprocal(ssb[:, gi, :], ssb[:, gi, :])
            for kc in range(NCH):
                ml.wait(v, kc)
                v.scalar_tensor_tensor(out=xin[:, kc, :], in0=mr[:, kc % 4, :], scalar=P.cols[:, cb + kc:cb + kc + 1],
                                       in1=ssb[:, kc // 8, :], op0=ALU.mult, op1=ALU.mult).then_inc(vd, 1)
    nc.all_engine_barrier()


def mixer(P, l, parts):
    nc = P.nc
    with ExitStack() as outer:
        xnT = outer.enter_context(_sbt(nc, "mx_xn", [128, NCH, TG], BF16))
        for tg in range(NTG):
            norm_block(P, tg, l * LCOLS + 32, xnT)
            inproj_block(P, tg, xnT, l)
    for kind in ("A", "C", "D"):
        if kind in parts:
            attn_block(P, l, kind)
    if "B" in parts:
        conv_block(P, l)
    import os
    stop = os.environ.get("K_STOP", "")
    if stop == "noout":
        return
    with ExitStack() as outer:
        xin = outer.enter_context(_sbt(nc, "mx_xin", [128, NCH, TG], BF16))
        for tg in range(NTG):
            outproj_prologue(P, tg, l, xin)
            if stop == "nogemm":
                continue
            gemm_resid_block(P, tg, xin, NCH, l, "wmo", l * 32, 1.0)


def final_phase(P, out):
    nc = P.nc
    NT = S // 128
    with ExitStack() as ps:
        hb = ps.enter_context(_sbt(nc, "fp_h", [128, NCH, 512], F32))
        sq = ps.enter_context(_sbt(nc, "fp_sq", [128, 2, 512], F32))
        rstd = ps.enter_context(_sbt(nc, "fp_rstd", [128, 512], F32))
        ot = ps.enter_context(_sbt(nc, "fp_o", [128, 2, D], F32))
        ssp = ps.enter_context(_pst(nc, "fp_ss", [128, 512], F32))
        tp = [ps.enter_context(_pst(nc, f"fp_tp{i}", [128, 4, 128], F32)) for i in range(4)]
        ld = ps.enter_context(_sem(nc, "fp_ld"))
        sqr = ps.enter_context(_sem(nc, "fp_sqr"))
        sqf = ps.enter_context(_sem(nc, "fp_sqf"))
        vd = ps.enter_context(_sem(nc, "fp_vd"))
        stp = ps.enter_context(_sem(nc, "fp_tp"))
        scp = ps.enter_context(_sem(nc, "fp_cp"))
        sst = SlotSem(nc, ps, "fp_st", 2)
        rs = ps.enter_context(_sem(nc, "fp_rs"))
        ps.callback(_end_block(nc))
        blk = ps.enter_context(nc.Block())
        hv = P.hT.rearrange("(c p) t -> p c t", p=128)
        NB = S // 512

        @blk.sync
        def _(s):
            for b in range(NB):
                if b >= 1:
                    s.wait_ge(scp, 32 * b)
                for q in range(4):
                    s.dma_start(out=hb[:, q * 8:(q + 1) * 8, :], in_=hv[:, q * 8:(q + 1) * 8, b * 512:(b + 1) * 512]).then_inc(ld, 16)
                for t4 in range(4):
                    tt = b * 4 + t4
                    s.wait_ge(scp, 8 * (tt + 1))
                    s.dma_start(out=out[tt * 128:(tt + 1) * 128, :], in_=ot[:, tt % 2, :]).then_inc(sst.sem(tt), 16)
            sst.wait(s, NT - 2)
            sst.wait(s, NT - 1)

        @blk.scalar
        def _(a):
            for b in range(NB):
                for c in range(NCH):
                    k = b * NCH + c
                    if c == 0:
                        a.wait_ge(ld, 64 * (b + 1))
                    if k >= 2:
                        a.wait_ge(sqf, k - 1)
                    a.activation(out=sq[:, k % 2, :], in_=hb[:, c, :], func=AF.Square).then_inc(sqr, 1)
                a.wait_ge(sqf, NCH * (b + 1))
                a.activation(out=rstd[:], in_=ssp[:], func=AF.Sqrt, bias=P.epsb[:], scale=1.0 / D).then_inc(rs, 1)

        @blk.tensor
        def _(t):
            for b in range(NB):
                if b >= 1:
                    t.wait_ge(vd, b)
                for c in range(NCH):
                    k = b * NCH + c
                    t.wait_ge(sqr, k + 1)
                    t.matmul(ssp[:], lhsT=P.onesf[:], rhs=sq[:, k % 2, :], start=(c == 0), stop=(c == NCH - 1)).then_inc(sqf, 1)
                t.wait_ge(vd, b + 1)
                for t4 in range(4):
                    tt = b * 4 + t4
                    for q in range(8):
                        k = tt * 8 + q
                        if k >= 4:
                            t.wait_ge(scp, k - 3)
                        for i in range(4):
                            c = q * 4 + i
                            ins = t.transpose(tp[k % 4][:, i, :], hb[:, c, t4 * 128:(t4 + 1) * 128], P.idf[:])
                        ins.then_inc(stp, 1)

        @blk.vector
        def _(v):
            for b in range(NB):
                v.wait_ge(rs, b + 1)
                v.reciprocal(rstd[:], rstd[:])
                for c in range(NCH):
                    ins = v.scalar_tensor_tensor(out=hb[:, c, :], in0=hb[:, c, :],
                                                 scalar=P.cols[:, COL_FINAL + c:COL_FINAL + c + 1], in1=rstd[:],
                                                 op0=ALU.mult, op1=ALU.mult)
                ins.then_inc(vd, 1)
                for t4 in range(4):
                    tt = b * 4 + t4
                    for q in range(8):
                        k = tt * 8 + q
                        v.wait_ge(stp, k + 1)
                        if q == 0 and tt >= 2:
                            sst.wait(v, tt - 2)
                        v.tensor_copy(ot[:, tt % 2, q * 512:(q + 1) * 512].rearrange("p (i c) -> p i c", i=4), tp[k % 4][:]).then_inc(scp, 1)
    nc.all_engine_barrier()


def _tile_in(W, n):
    return np.ascontiguousarray(W.reshape(32, 128, n, 128).transpose(2, 1, 0, 3)).reshape(n * 128, 4096)


def _tile_ffn_in(W):
    a = W.reshape(32, 128, 2, 48, 128).transpose(3, 2, 1, 0, 4)
    return np.ascontiguousarray(a).reshape(96 * 128, 4096)


def _tile_ffn_out(W):
    a = W.reshape(2, 24, 128, 32, 128).transpose(0, 3, 2, 1, 4)
    return np.ascontiguousarray(a).reshape(64 * 128, 3072)


def host_consts():
    kl = np.arange(128)[:, None]
    d = (np.arange(TABW)[None, :] - TABOFF) - kl
    ad = np.abs(d)
    dist = ad.astype(np.float32)
    maskc = (ad <= 128).astype(np.float32)
    maskd = ((ad <= 64).astype(np.float32) + ((d % 4 == 0) & (ad <= 256)).astype(np.float32)
             + ((d % 16 == 0) & (ad <= 1024)).astype(np.float32))
    maska = np.zeros((128, 4, 8, 512), np.float32)
    rk, ck = np.arange(128) // 64, np.arange(128) % 64
    for g in range(4):
        j0 = NA_J0[g]
        qtok = g * 512 + np.arange(512)
        r, c = qtok // 64, qtok % 64
        rs = np.clip(r - 4, 0, 24)
        cs = np.clip(c - 8, 0, 48)
        for jj in range(NA_NJ[g]):
            j = j0 + jj
            kr = 2 * j + rk
            okr = (kr[:, None] >= rs[None, :]) & (kr[:, None] < rs[None, :] + 8)
            okc = (ck[:, None] >= cs[None, :]) & (ck[:, None] < cs[None, :] + 16)
            maska[:, g, jj, :] = (okr & okc)
    bf = ml_dtypes.bfloat16
    return dist, maskc.astype(bf), maskd.astype(bf), maska.reshape(128, -1).astype(bf)


NA_J0 = [0, 2, 6, 10]
NA_NJ = [6, 8, 8, 6]


def host_rtab(rpb):
    L = rpb.shape[0]
    rk, ck = np.arange(128) // 64, np.arange(128) % 64
    delta = 5 - np.arange(11)
    dr = 2 * delta[None, :, None] + rk[:, None, None] - rk[None, None, :]
    dc = ck[:, None, None] - ck[None, None, :] + 0 * delta[None, :, None]
    ir = np.clip(dr + 7, 0, 14)
    ic = np.clip(dc + 15, 0, 30)
    t = rpb[:, :, ir, ic]
    return np.ascontiguousarray(t).reshape(L * 8 * 128, 11 * 128).astype(np.float32)


def host_cols(inp, L):
    cols = np.zeros((128, NCOLS), np.float32)
    fm = lambda v: v.reshape(-1, 128).T
    for l in range(L):
        b = l * LCOLS
        cols[:, b:b + 32] = fm(inp["ffn1_norm"][l])
        cols[:, b + 32:b + 64] = fm(inp["mix_norm"][l])
        cols[:, b + 64:b + 96] = fm(inp["ffn2_norm"][l])
        cols[:, b + 96:b + 128] = fm(inp["branch_norm"][l].reshape(-1))
        cols[:, b + 128:b + 136] = fm(inp["conv_b"][l])
        cols[:, b + 136:b + 144] = fm(inp["conv_ln_g"][l])
        cols[:, b + 144:b + 152] = fm(inp["conv_ln_b"][l])
        cw = inp["conv_w"][l]
        cols[:, b + 152:b + 400] = cw.T.reshape(8, 128, 31).transpose(1, 0, 2).reshape(128, 248)
        cols[:, COL_SINK + l * 8:COL_SINK + l * 8 + 8] = inp["swa_sink"][l][None, :]
    cols[:, COL_FINAL:COL_FINAL + 32] = fm(inp["final_norm"])
    return cols


def host_prepare(inp, L):
    shared = {}
    shared["w1i"] = np.concatenate([_tile_ffn_in(inp["ffn1_w_in"][l]) for l in range(L)], 0)
    shared["w1o"] = np.concatenate([_tile_ffn_out(inp["ffn1_w_out"][l]) for l in range(L)], 0)
    shared["wmi"] = np.concatenate([_tile_in(inp["w_in"][l], 76) for l in range(L)], 0)
    shared["wmo"] = np.concatenate([_tile_in(inp["w_out"][l], 32) for l in range(L)], 0)
    shared["w2i"] = np.concatenate([_tile_ffn_in(inp["ffn2_w_in"][l]) for l in range(L)], 0)
    shared["w2o"] = np.concatenate([_tile_ffn_out(inp["ffn2_w_out"][l]) for l in range(L)], 0)
    shared["cols"] = host_cols(inp, L)
    shared["ident"] = np.eye(128, dtype=np.float32)
    dist, maskc, maskd, maska = host_consts()
    shared["dist"], shared["maskc"], shared["maskd"], shared["maska"] = dist, maskc, maskd, maska
    shared["rtab"] = host_rtab(inp["na_rpb"][:L])
    return shared


def kernel(**inputs):
    inp = {k: np.asarray(v) for k, v in inputs.items()}
    L = 4
    B = inp["x"].shape[0]
    n = N_CORES
    per = B // n
    nc = build_program(L, nseq=per)
    shared = host_prepare(inp, L)
    in_maps = []
    for c in range(n):
        m = dict(shared)
        m["x"] = np.ascontiguousarray(inp["x"][c * per:(c + 1) * per]).reshape(per * S, D)
        in_maps.append(m)
    res = run_bass_kernel_spmd(nc, in_maps, core_ids=list(range(n)))
    outs = [np.asarray(r["out"]).reshape(per, S, D) for r in res.results]
    return np.concatenate(outs, axis=0).astype(np.float32)
```
